# Optimizing a Trainium2 kernel written in Bass

```python
import math
import jax, jax.numpy as jnp
from jax import lax
import numpy as np

D_MODEL = 1024
BATCH = 32
SEQ = 2048
DEPTH = 4
DEC_BATCH = 16
DEC_SEQ = 2048
PAST_LEN = 128

MIX_HALF = D_MODEL // 2
SHORT_K = 3
CONF_K = 31
D_FF = 4 * D_MODEL
FILTER_ORDER = 64
FILTER_BANDS = 16
FILTER_EMB = 1 + 2 * FILTER_BANDS
DECAY_TARGET = 1e-2
FAST_DECAY_PCT = 0.3
SLOW_DECAY_PCT = 1.5
N_EVEN = (DEPTH + 1) // 2
N_ODD = DEPTH // 2
ALPHA = (2 * DEPTH) ** 0.25
BETA = (8 * DEPTH) ** -0.25
LN_EPS = 1e-5

kernel_name = 'hybrid_shortconv_hyena_conformer_encoder'


def layer_norm(x, g, b):
    xf = x.astype(jnp.float32)
    mu = jnp.mean(xf, axis=-1, keepdims=True)
    xc = xf - mu
    var = jnp.mean(xc * xc, axis=-1, keepdims=True)
    return (xc * lax.rsqrt(var + LN_EPS) * g.astype(jnp.float32) + b.astype(jnp.float32)).astype(x.dtype)


def dwconv(x, w):
    K, C = w.shape
    return lax.conv_general_dilated(
        x, w.astype(x.dtype)[:, None, :], window_strides=(1,), padding=[(K // 2, K // 2)],
        dimension_numbers=('NWC', 'WIO', 'NWC'), feature_group_count=C)


def hyena_filter(L, w1, b1, w2, b2, w3, b3, w4, freq):
    f32 = jnp.float32
    t = jnp.linspace(0.0, 1.0, L, dtype=f32)[:, None]
    w = (2.0 * math.pi / L) * jnp.arange(L, dtype=f32)[:, None]
    bands = jnp.linspace(1e-4, FILTER_BANDS - 1, FILTER_BANDS, dtype=f32)[None, :]
    z = jnp.concatenate([t, jnp.cos(bands * w), -jnp.sin(bands * w)], axis=-1)
    fr = freq.astype(f32)
    h = jnp.sin(fr * (z @ w1.astype(f32) + b1.astype(f32)))
    h = jnp.sin(fr * (h @ w2.astype(f32) + b2.astype(f32)))
    h = jnp.sin(fr * (h @ w3.astype(f32) + b3.astype(f32)))
    h = h @ w4.astype(f32)
    max_decay = math.log(DECAY_TARGET) / FAST_DECAY_PCT
    min_decay = math.log(DECAY_TARGET) / SLOW_DECAY_PCT
    deltas = jnp.abs(jnp.linspace(min_decay, max_decay, MIX_HALF, dtype=f32))
    decay = jnp.exp(-t * deltas[None, :])
    h_fwd = h[:, :MIX_HALF] * decay
    h_bwd = h[:, MIX_HALF:] * decay
    k = jnp.concatenate([h_fwd, jnp.zeros((1, MIX_HALF), f32), h_bwd[:0:-1]], axis=0)
    return k / jnp.sum(jnp.abs(k), axis=0, keepdims=True)


def long_conv(u, k):
    L = u.shape[1]
    U = jnp.fft.rfft(u.astype(jnp.float32), n=2 * L, axis=1)
    Kf = jnp.fft.rfft(k, axis=0)
    return jnp.fft.irfft(U * Kf[None], n=2 * L, axis=1)[:, :L]


def even_mixer(x, w_in, conv_a, short_w, short_b, fw1, fb1, fw2, fb2, fw3, fb3, fw4, freq, hy_bias, w_out):
    H = MIX_HALF
    p = x @ w_in
    a_b, a_c, a_h, hy = p[..., :H], p[..., H:2 * H], p[..., 2 * H:3 * H], p[..., 3 * H:]
    y_a = a_b * dwconv(a_c * a_h, conv_a)
    hy = dwconv(hy, short_w) + short_b
    x0, x1, v = jnp.split(hy, 3, axis=-1)
    u = x1 * v
    k = hyena_filter(x.shape[1], fw1, fb1, fw2, fb2, fw3, fb3, fw4, freq)
    y_b = x0 * (long_conv(u, k).astype(u.dtype) + hy_bias * u)
    return jnp.concatenate([y_a, y_b], axis=-1) @ w_out


def conformer_conv(x, w_pw1, b_pw1, dw_w, dw_b, ln_g, ln_b, w_pw2, b_pw2):
    h = x @ w_pw1 + b_pw1
    a, g = jnp.split(h, 2, axis=-1)
    h = a * jax.nn.sigmoid(g)
    h = dwconv(h, dw_w) + dw_b
    h = jax.nn.silu(layer_norm(h, ln_g, ln_b))
    return h @ w_pw2 + b_pw2


def sq_relu_mlp(x, w1, w2):
    return jnp.square(jax.nn.relu(x @ w1)) @ w2


def setup_inputs(seed: int = 0) -> dict:
    key = jax.random.key(seed)
    ks = iter(jax.random.split(key, 48))
    D, H = D_MODEL, MIX_HALF

    def nrm(shape, scale):
        return jax.random.normal(next(ks), shape, jnp.float32) * scale

    return {
        'x_prompt': nrm((BATCH, SEQ, D), 1.0),
        'x_sample': nrm((DEC_BATCH, DEC_SEQ, D), 1.0),
        'e_w_in': nrm((N_EVEN, D, 6 * H), D ** -0.5),
        'e_conv_a': nrm((N_EVEN, SHORT_K, H), SHORT_K ** -0.5),
        'e_short_w': nrm((N_EVEN, SHORT_K, 3 * H), SHORT_K ** -0.5),
        'e_short_b': nrm((N_EVEN, 3 * H), 0.02),
        'e_flt_w1': nrm((N_EVEN, FILTER_EMB, FILTER_ORDER), FILTER_EMB ** -0.5),
        'e_flt_b1': nrm((N_EVEN, FILTER_ORDER), 0.02),
        'e_flt_w2': nrm((N_EVEN, FILTER_ORDER, FILTER_ORDER), FILTER_ORDER ** -0.5),
        'e_flt_b2': nrm((N_EVEN, FILTER_ORDER), 0.02),
        'e_flt_w3': nrm((N_EVEN, FILTER_ORDER, FILTER_ORDER), FILTER_ORDER ** -0.5),
        'e_flt_b3': nrm((N_EVEN, FILTER_ORDER), 0.02),
        'e_flt_w4': nrm((N_EVEN, FILTER_ORDER, 2 * H), FILTER_ORDER ** -0.5),
        'e_flt_freq': 1.0 + nrm((N_EVEN, FILTER_ORDER), 0.01),
        'e_hy_bias': nrm((N_EVEN, H), 0.1),
        'e_w_out': nrm((N_EVEN, D, D), BETA * D ** -0.5),
        'o_w_pw1': nrm((N_ODD, D, 2 * D), D ** -0.5),
        'o_b_pw1': nrm((N_ODD, 2 * D), 0.02),
        'o_dw_w': nrm((N_ODD, CONF_K, D), CONF_K ** -0.5),
        'o_dw_b': nrm((N_ODD, D), 0.02),
        'o_ln_g': 1.0 + nrm((N_ODD, D), 0.02),
        'o_ln_b': nrm((N_ODD, D), 0.02),
        'o_w_pw2': nrm((N_ODD, D, D), BETA * D ** -0.5),
        'o_b_pw2': nrm((N_ODD, D), 0.02),
        'ln1_g': 1.0 + nrm((DEPTH, D), 0.02),
        'ln1_b': nrm((DEPTH, D), 0.02),
        'mlp_w1': nrm((DEPTH, D, D_FF), D ** -0.5),
        'mlp_w2': nrm((DEPTH, D_FF, D), BETA * D_FF ** -0.5),
        'ln2_g': 1.0 + nrm((DEPTH, D), 0.02),
        'ln2_b': nrm((DEPTH, D), 0.02),
    }


def reference(x_prompt, x_sample, e_w_in, e_conv_a, e_short_w, e_short_b, e_flt_w1, e_flt_b1,
              e_flt_w2, e_flt_b2, e_flt_w3, e_flt_b3, e_flt_w4, e_flt_freq, e_hy_bias, e_w_out,
              o_w_pw1, o_b_pw1, o_dw_w, o_dw_b, o_ln_g, o_ln_b, o_w_pw2, o_b_pw2,
              ln1_g, ln1_b, mlp_w1, mlp_w2, ln2_g, ln2_b):
    def run(x):
        for i in range(DEPTH):
            j = i // 2
            if i % 2 == 0:
                m = even_mixer(x, e_w_in[j], e_conv_a[j], e_short_w[j], e_short_b[j],
                               e_flt_w1[j], e_flt_b1[j], e_flt_w2[j], e_flt_b2[j],
                               e_flt_w3[j], e_flt_b3[j], e_flt_w4[j], e_flt_freq[j],
                               e_hy_bias[j], e_w_out[j])
            else:
                m = conformer_conv(x, o_w_pw1[j], o_b_pw1[j], o_dw_w[j], o_dw_b[j],
                                   o_ln_g[j], o_ln_b[j], o_w_pw2[j], o_b_pw2[j])
            x = layer_norm(ALPHA * x + m, ln1_g[i], ln1_b[i])
            x = layer_norm(ALPHA * x + sq_relu_mlp(x, mlp_w1[i], mlp_w2[i]), ln2_g[i], ln2_b[i])
        return x

    y_prompt = run(x_prompt)
    y_sample = run(x_sample)
    return (y_prompt, y_sample)
```

```python
import math
import numpy as np
import ml_dtypes
import concourse.bass as bass
import concourse.mybir as mybir
from concourse.bass_utils import run_bass_kernel_spmd
from contextlib import ExitStack

F32 = mybir.dt.float32
BF16 = mybir.dt.bfloat16
AF = mybir.ActivationFunctionType
ALU = mybir.AluOpType

D = 1024
T = 2048
H = 512
DFF = 4096
NCORES = 8
NSEQ_TOTAL = 48
ALPHA = float(8 ** 0.25)
EPS = 1e-5
NFFT = 4096
CONF_K = 31
PADC = 15

ENGS = ["pe", "act", "dve", "pool", "sp"]

OFF_X32 = 0
OFF_XB = 65536
OFF_C = 98304
OFF_C0, OFF_C1, OFF_C2, OFF_C3 = OFF_C, OFF_C + 16384, OFF_C + 32768, OFF_C + 49152
WR_SLOTS = 5
CR_SLOTS = 4
OFF_WR = 163840
OFF_CR = OFF_WR + WR_SLOTS * 4096
OFF_TT = OFF_CR + CR_SLOTS * 4096
TT_BYTES = 5120
OFF_K = OFF_TT + TT_BYTES
NCOL = 864
OFF_ID = OFF_K + NCOL * 4
OFF_ONESB = OFF_ID + 256
OFF_ONESF = OFF_ONESB + 256
ARENA_BYTES = OFF_ONESF + 512
assert ARENA_BYTES <= 210944, ARENA_BYTES


class V:
    __slots__ = ("ap", "res")

    def __init__(self, ap, res):
        self.ap = ap
        self.res = frozenset(res)


def _prod(s):
    r = 1
    for x in s:
        r *= x
    return r


class Buf:
    def __init__(self, arena, off, dtype, shape):
        self.esz = 4 if dtype == F32 else 2
        self.off = off
        self.shape = tuple(shape)
        n = _prod(shape)
        assert off % 4 == 0 or self.esz == 2
        a = arena[:, off // 2: off // 2 + n * self.esz // 2]
        if dtype == F32:
            a = a.bitcast(F32)
        if len(shape) == 2:
            a = a.rearrange("p (a b) -> p a b", a=shape[0])
        elif len(shape) == 3:
            a = a.rearrange("p (a b c) -> p a b c", a=shape[0], b=shape[1])
        self.a = a
        st = []
        s = self.esz
        for d in reversed(shape):
            st.append(s)
            s *= d
        self.strides = tuple(reversed(st))
        self.nbytes = n * self.esz

    def __call__(self, *idx, parts=None):
        idx = list(idx) + [None] * (len(self.shape) - len(idx))
        key = [slice(None) if parts is None else slice(parts[0], parts[1])]
        rngs = []
        for d, ix in enumerate(idx):
            if ix is None:
                key.append(slice(None))
                rngs.append((0, self.shape[d]))
            elif isinstance(ix, tuple):
                key.append(slice(ix[0], ix[1]))
                rngs.append((ix[0], ix[1]))
            else:
                key.append(ix)
                rngs.append((ix, ix + 1))
        ap = self.a[tuple(key)]
        res = set()
        outer = rngs[:-1]
        lo, hi = rngs[-1]

        def rec(d, base):
            if d == len(outer):
                b0 = base + lo * self.strides[-1]
                b1 = base + hi * self.strides[-1] - 1
                for blk in range(b0 // 1024, b1 // 1024 + 1):
                    res.add(("sb", blk))
                return
            for i in range(outer[d][0], outer[d][1]):
                rec(d + 1, base + i * self.strides[d])

        rec(0, self.off)
        return V(ap, res)


class Prog:
    def __init__(self):
        self.ins = []
        self.cnt = {e: 0 for e in ENGS}
        self.lastw = {}
        self.rds = {}
        self.keyidx = {("e", e): i for i, e in enumerate(ENGS)}
        self.dcnt = {}
        self.last_vc = {e: None for e in ENGS}
        self.tokvc = {}
        self.nk = 72

    def dma_sem(self, name):
        if name not in self.dcnt:
            self.dcnt[name] = 0
            self.keyidx[("d", name)] = len(self.keyidx)
            assert len(self.keyidx) <= self.nk
        return name

    def add(self, eng, fn, reads=(), writes=(), dsem=None):
        rres = set()
        for v in reads:
            rres |= v.res
        wres = set()
        for v in writes:
            wres |= v.res
        deps = {}
        for r in rres:
            t = self.lastw.get(r)
            if t is not None:
                deps[t] = True
        for w in wres:
            t = self.lastw.get(w)
            if t is not None and t not in deps:
                deps[t] = False
            rd = self.rds.get(w)
            if rd:
                for t in rd.values():
                    if t not in deps:
                        deps[t] = False
        self.cnt[eng] += 1
        n = self.cnt[eng]
        fl = []
        for t, raw in deps.items():
            if t[0] == "e" and t[1] == eng:
                if eng == "pe" or not raw:
                    continue
            fl.append(t)
        vc = np.zeros(self.nk, np.int64) if self.last_vc[eng] is None else self.last_vc[eng].copy()
        for t in fl:
            np.maximum(vc, self.tokvc[t], out=vc)
        mytok = ("e", eng, n)
        vc_issue = vc
        vc_e = vc.copy()
        vc_e[self.keyidx[("e", eng)]] = n
        self.last_vc[eng] = vc_issue
        self.tokvc[mytok] = vc_e
        if dsem is not None:
            self.dcnt[dsem] += 1
            tok = ("d", dsem, 16 * self.dcnt[dsem])
            vcd = vc.copy()
            vcd[self.keyidx[("d", dsem)]] = tok[2]
            self.tokvc[tok] = vcd
        else:
            tok = mytok
            self.last_vc[eng] = vc_issue
        for r in rres:
            d = self.rds.setdefault(r, {})
            k = (tok[0], tok[1]) if tok[0] == "e" else tok
            d[k] = tok
        for w in wres:
            self.lastw[w] = tok
            self.rds[w] = {}
        self.ins.append({"eng": eng, "fn": fn, "deps": fl, "tok": mytok, "dsem": dsem})
        return tok

    def emit(self, nc, block, esems, dsems):
        sig = {e: set() for e in ENGS}
        for I in self.ins:
            for t in I["deps"]:
                if t[0] == "e":
                    sig[t[1]].add(t[2])
        rank = {}
        for e in ENGS:
            r = {}
            for i, n in enumerate(sorted(sig[e])):
                r[n] = i + 1
            rank[e] = r
        per = {e: [] for e in ENGS}
        for I in self.ins:
            per[I["eng"]].append(I)
        keyidx = self.keyidx
        tokvc = self.tokvc
        stats = {"waits": 0}

        def run(e, engobj):
            known = np.zeros(self.nk, np.int64)
            for I in per[e]:
                deps = sorted(I["deps"], key=lambda t: -int(tokvc[t].sum()))
                for t in deps:
                    ki = keyidx[(t[0], t[1])]
                    if known[ki] >= t[2]:
                        continue
                    if t[0] == "e":
                        engobj.wait_ge(esems[t[1]], rank[t[1]][t[2]])
                    else:
                        engobj.wait_ge(dsems[t[1]], t[2])
                    stats["waits"] += 1
                    np.maximum(known, tokvc[t], out=known)
                if I["fn"] is None:
                    continue
                inst = I["fn"](engobj)
                if I["dsem"] is not None:
                    inst.then_inc(dsems[I["dsem"]], 16)
                elif I["tok"][2] in sig[e]:
                    inst.then_inc(esems[e], 1)

        @block.tensor
        def _(t):
            run("pe", t)

        @block.scalar
        def _(s):
            run("act", s)

        @block.vector
        def _(v):
            run("dve", v)

        @block.gpsimd
        def _(g):
            run("pool", g)

        @block.sync
        def _(s):
            run("sp", s)

        return stats


def _pair_units(W, chunks):
    K = W.shape[0]
    assert K == 1024 and len(chunks) % 2 == 0
    Wr = W.reshape(8, 128, -1, 128)
    sel = Wr[:, :, chunks, :]
    sel = sel.transpose(2, 1, 0, 3)
    n = len(chunks)
    sel = sel.reshape(n // 2, 2, 128, 8, 128).transpose(0, 2, 1, 3, 4)
    return np.ascontiguousarray(sel).reshape(n // 2, 128, 2048)


def _w2_units(W2, half):
    Wh = W2[half * 2048:(half + 1) * 2048].reshape(16, 128, 8, 128)
    return np.ascontiguousarray(Wh.transpose(2, 1, 0, 3)).reshape(8, 128, 2048)


HY_ORDER = []
for _c in range(4):
    HY_ORDER += [16 + _c, 20 + _c, 12 + _c]
MA_ORDER = []
for _c in range(4):
    MA_ORDER += [4 + _c, 8 + _c, 0 + _c]
PW1_ORDER = []
for _c in range(8):
    PW1_ORDER += [8 + _c, _c]


def pack_weights(inp):
    units = []
    for l in range(4):
        j = l // 2
        if l % 2 == 0:
            w_in = inp["e_w_in"][j]
            units.append(_pair_units(w_in, MA_ORDER))
            units.append(_pair_units(w_in, HY_ORDER))
            units.append(_pair_units(inp["e_w_out"][j], list(range(8))))
        else:
            units.append(_pair_units(inp["o_w_pw1"][j], PW1_ORDER))
            units.append(_pair_units(inp["o_w_pw2"][j], list(range(8))))
        w1 = inp["mlp_w1"][l]
        w2 = inp["mlp_w2"][l]
        for half in range(2):
            units.append(_pair_units(w1, list(range(half * 16, half * 16 + 16))))
            units.append(_w2_units(w2, half))
    return np.concatenate(units, axis=0).astype(np.float32)


def dft_tables():
    n = np.arange(T, dtype=np.float64)
    f = np.arange(T, dtype=np.float64)
    th = 2.0 * np.pi * (f + 0.5) / NFFT
    ang = np.outer(n, th)
    Fre = np.cos(ang)
    Fim = -np.sin(ang)
    units = []
    for fc in range(16):
        for M in (Fre, Fim):
            blk = M[:, fc * 128:(fc + 1) * 128].reshape(16, 128, 128).transpose(1, 0, 2)
            units.append(blk.reshape(128, 2048))
    s = 2.0 / NFFT
    Gre = (s * Fre).T
    Gim = (s * Fim).T
    for h in range(2):
        for fc in range(16):
            u = np.stack([Gre[fc * 128:(fc + 1) * 128, h * 1024:(h + 1) * 1024],
                          Gim[fc * 128:(fc + 1) * 128, h * 1024:(h + 1) * 1024]], axis=1)
            units.append(u.reshape(128, 2048))
    return np.stack(units).astype(ml_dtypes.bfloat16)


def filter_consts():
    L = T
    t = np.linspace(0.0, 1.0, L, dtype=np.float32)[:, None]
    w = (np.float32(2.0 * math.pi / L)) * np.arange(L, dtype=np.float32)[:, None]
    bands = np.linspace(1e-4, 15, 16, dtype=np.float32)[None, :]
    z = np.concatenate([t, np.cos(bands * w), -np.sin(bands * w)], axis=-1).astype(np.float32)
    max_decay = math.log(1e-2) / 0.3
    min_decay = math.log(1e-2) / 1.5
    deltas = np.abs(np.linspace(min_decay, max_decay, H, dtype=np.float32))
    decay = np.exp(-t * deltas[None, :]).astype(np.float32)
    zT = np.ascontiguousarray(z.T)
    dec = np.ascontiguousarray(decay.reshape(16, 128, H).transpose(1, 0, 2))
    return zT, dec


class Cols:
    def __init__(self):
        self.n = 0
        self.idx = {}
        self.data = []

    def add(self, key, vec):
        self.idx[key] = self.n
        self.n += 1
        self.data.append(np.asarray(vec, np.float32).reshape(128))

    def arr(self):
        a = np.zeros((128, NCOL), np.float32)
        assert self.n <= NCOL, self.n
        a[:, :self.n] = np.stack(self.data, axis=1)
        return a


def build_cols(inp):
    c = Cols()

    def chunks(name, vec, n):
        v = np.asarray(vec, np.float32).reshape(n, 128)
        for o in range(n):
            c.add((name, o), v[o])

    for l in range(4):
        chunks(("ln1g", l), inp["ln1_g"][l], 8)
        chunks(("ln1b", l), inp["ln1_b"][l], 8)
        chunks(("ln2g", l), inp["ln2_g"][l], 8)
        chunks(("ln2b", l), inp["ln2_b"][l], 8)
    for j in range(2):
        for k in range(3):
            chunks(("conva", j, k), inp["e_conv_a"][j][k], 4)
            chunks(("shortw", j, k), inp["e_short_w"][j][k], 12)
        chunks(("shortb", j), inp["e_short_b"][j], 12)
        chunks(("bpw1", j), inp["o_b_pw1"][j], 16)
        chunks(("dwb", j), inp["o_dw_b"][j], 8)
        chunks(("olng", j), inp["o_ln_g"][j], 8)
        chunks(("olnb", j), inp["o_ln_b"][j], 8)
        chunks(("bpw2", j), inp["o_b_pw2"][j], 8)
        for o in range(8):
            for k in range(CONF_K):
                c.add(("dww", j, o, k), np.asarray(inp["o_dw_w"][j][k], np.float32).reshape(8, 128)[o])
        for nm, key in (("fr", "e_flt_freq"), ("fb1", "e_flt_b1"), ("fb2", "e_flt_b2"), ("fb3", "e_flt_b3")):
            v = np.zeros(128, np.float32)
            v[:64] = inp[key][j]
            c.add((nm, j), v)
        for nm in ("frb1", "frb2", "frb3"):
            c.add((nm, j), np.zeros(128, np.float32))
    c.add(("neghalf",), np.full(128, -0.5, np.float32))
    return c


class Ring:
    def __init__(self, kb, name, eng, off, nslots, units):
        self.kb = kb
        self.name = name
        self.eng = eng
        self.n = nslots
        self.units = units
        self.slots = [Buf(kb.arena, off + i * 4096, BF16, (2048,)) for i in range(nslots)]
        self.sems = [kb.prog.dma_sem(f"{name}{i}") for i in range(nslots)]
        self.next_load = 0
        self.next_use = 0

    def _load(self, idx):
        if idx >= len(self.units):
            return
        s = idx % self.n
        src = self.units[idx]
        n = src.ap.shape[-1]
        dst = self.slots[s]((0, n))
        self.kb.dma(self.eng, dst, src, self.sems[s])

    def acquire(self):
        if self.next_use == 0:
            for i in range(self.n):
                self._load(i)
            self.next_load = self.n
        idx = self.next_use
        self.next_use += 1
        return idx

    def slot(self, idx):
        return self.slots[idx % self.n]

    def release(self, idx):
        self._load(idx + self.n)


class KB:
    def __init__(self, nseq, nlayers=4, do_prologue=True, dbg=None):
        self.nseq = nseq
        self.nlayers = nlayers
        self.dbg = dbg
        self.do_prologue = do_prologue
        self.prog = Prog()
        self.nc = bass.Bass("TRN2", target_bir_lowering=False)
        nc = self.nc
        self.x_d = nc.dram_tensor("x", [nseq, 128, 8, T], F32, kind="ExternalInput").ap()
        self.y_d = nc.dram_tensor("y", [nseq, 128, 8, T], F32, kind="ExternalOutput").ap()
        self.w_d = nc.dram_tensor("wts", [184, 128, 2048], F32, kind="ExternalInput").ap()
        self.dft_d = nc.dram_tensor("dft", [64, 128, 2048], BF16, kind="ExternalInput").ap()
        self.cols_d = nc.dram_tensor("cols", [128, NCOL], F32, kind="ExternalInput").ap()
        self.ident_d = nc.dram_tensor("ident", [128, 128], BF16, kind="ExternalInput").ap()
        self.zT_d = nc.dram_tensor("zT", [33, T], F32, kind="ExternalInput").ap()
        self.dec_d = nc.dram_tensor("dec", [128, 16, H], F32, kind="ExternalInput").ap()
        self.fw1_d = nc.dram_tensor("fw1", [2, 33, 64], F32, kind="ExternalInput").ap()
        self.fw2_d = nc.dram_tensor("fw2", [2, 64, 64], F32, kind="ExternalInput").ap()
        self.fw3_d = nc.dram_tensor("fw3", [2, 64, 64], F32, kind="ExternalInput").ap()
        self.fw4_d = nc.dram_tensor("fw4", [2, 64, 1024], F32, kind="ExternalInput").ap()
        self.hyb_d = nc.dram_tensor("hyb", [2, 1, H], F32, kind="ExternalInput").ap()
        self.kf_d = nc.dram_tensor("kfs", [2, 16, 128, 1024], BF16, kind="Internal").ap()

    def mm(self, out, lhsT, rhs, start, stop):
        self.prog.add("pe", lambda e, o=out.ap, l=lhsT.ap, r=rhs.ap, s=start, p=stop:
                      e.matmul(o, lhsT=l, rhs=r, start=s, stop=p), reads=[lhsT, rhs], writes=[out])

    def transpose(self, out, in_, ident):
        self.prog.add("pe", lambda e, o=out.ap, i=in_.ap, d=ident.ap: e.transpose(out=o, in_=i, identity=d),
                      reads=[in_, ident], writes=[out])

    def act(self, out, in_, func, bias=None, scale=None, extra_reads=()):
        kw = {}
        rd = [in_] + list(extra_reads)
        if bias is not None:
            if isinstance(bias, V):
                kw["bias"] = bias.ap
                rd.append(bias)
            else:
                kw["bias"] = bias
        if scale is not None:
            if isinstance(scale, V):
                kw["scale"] = scale.ap
                rd.append(scale)
            else:
                kw["scale"] = scale
        self.prog.add("act", lambda e, o=out.ap, i=in_.ap, f=func, kw=kw: e.activation(out=o, in_=i, func=f, **kw),
                      reads=rd, writes=[out])

    def tt(self, eng, out, in0, in1, op):
        self.prog.add(eng, lambda e, o=out.ap, a=in0.ap, b=in1.ap, op=op: e.tensor_tensor(out=o, in0=a, in1=b, op=op),
                      reads=[in0, in1], writes=[out])

    def ts(self, eng, out, in0, s1, op0, s2=None, op1=None):
        rd = [in0]
        a1 = s1
        a2 = s2
        if isinstance(s1, V):
            rd.append(s1)
            a1 = s1.ap
        if isinstance(s2, V):
            rd.append(s2)
            a2 = s2.ap
        if op1 is None:
            fn = lambda e, o=out.ap, a=in0.ap: e.tensor_scalar(out=o, in0=a, scalar1=a1, scalar2=None, op0=op0)
        else:
            fn = lambda e, o=out.ap, a=in0.ap: e.tensor_scalar(out=o, in0=a, scalar1=a1, scalar2=a2, op0=op0, op1=op1)
        self.prog.add(eng, fn, reads=rd, writes=[out])

    def stt(self, out, in0, scalar, in1, op0, op1):
        rd = [in0, in1]
        sc = scalar
        if isinstance(scalar, V):
            rd.append(scalar)
            sc = scalar.ap
        self.prog.add("dve", lambda e, o=out.ap, a=in0.ap, b=in1.ap: e.scalar_tensor_tensor(
            out=o, in0=a, scalar=sc, in1=b, op0=op0, op1=op1), reads=rd, writes=[out])

    def copy(self, eng, out, in_):
        if eng == "act":
            self.act(out, in_, AF.Copy)
        else:
            self.prog.add(eng, lambda e, o=out.ap, i=in_.ap: e.tensor_copy(out=o, in_=i), reads=[in_], writes=[out])

    def memset(self, eng, out, val):
        self.prog.add(eng, lambda e, o=out.ap: e.memset(o, val), reads=[], writes=[out])

    def recip(self, out, in_):
        self.prog.add("dve", lambda e, o=out.ap, i=in_.ap: e.reciprocal(out=o, in_=i), reads=[in_], writes=[out])

    def dma(self, eng, out, in_, sem):
        return self.prog.add(eng, lambda e, o=out.ap, i=in_.ap: e.dma_start(out=o, in_=i), reads=[in_], writes=[out], dsem=sem)

    def PS(self, b, nb=1, lo=0, hi=None, parts=None):
        hi = nb * 512 if hi is None else hi
        ap = self.ps[:, b * 512 + lo: b * 512 + hi] if parts is None else self.ps[parts[0]:parts[1], b * 512 + lo: b * 512 + hi]
        return V(ap, {("ps", bb) for bb in range(b + lo // 512, b + (hi - 1) // 512 + 1)})

    def PSB(self, b, lo, hi):
        ap = self.ps[:, b * 512:(b + 1) * 512].bitcast(BF16)[:, lo:hi]
        return V(ap, {("ps", b)})

    def col(self, key, parts=None):
        i = self.cols.idx[key]
        return self.KC((i, i + 1), parts=parts)

    def DV(self, ap, res=()):
        return V(ap, set(res))

    def build(self, cols):
        self.cols = cols
        nc = self.nc
        with ExitStack() as st:
            self.arena = st.enter_context(nc.sbuf_tensor("arena", [128, ARENA_BYTES // 2], BF16))
            self.ps = st.enter_context(nc.psum_tensor("ps", [128, 4096], F32))
            ar = self.arena
            self.X32 = Buf(ar, OFF_X32, F32, (8, T))
            self.XB = Buf(ar, OFF_XB, BF16, (8, T))
            self.KC = Buf(ar, OFF_K, F32, (NCOL,))
            self.IDN = Buf(ar, OFF_ID, BF16, (128,))
            self.ONESB = Buf(ar, OFF_ONESB, BF16, (128,))
            self.ONESF = Buf(ar, OFF_ONESF, F32, (128,))
            self.record()
            esems = {e: st.enter_context(nc.semaphore(f"e_{e}")) for e in ENGS}
            dsems = {n: st.enter_context(nc.semaphore(f"d_{n}")) for n in self.prog.dcnt}
            block = st.enter_context(nc.Block())
            self.stats = self.prog.emit(nc, block, esems, dsems)
        return nc

    def dump(self, stage, off):
        if self.dbg != stage:
            return
        b = Buf(self.arena, off, F32, (8, T))
        for o in range(8):
            self.dma("sp", V(self.y_d[0, :, o, :], {("ydram", 0, o)}), b(o), self.prog.dma_sem(f"ys{o}"))
        raise StopIteration

    def record(self):
        try:
            self.record_()
        except StopIteration:
            pass
        self.prog.add("sp", None, reads=[V(None, {("ydram", s, o) for s in range(self.nseq) for o in range(8)})])

    def record_(self):
        P = self.prog
        csem = P.dma_sem("const")
        self.dma("sp", self.KC(), self.DV(self.cols_d), csem)
        self.dma("sp", self.IDN(), self.DV(self.ident_d), csem)
        self.memset("dve", self.ONESB(), 1.0 / 1024.0)
        self.memset("dve", self.ONESF(), 1.0)
        wunits = []
        cunits = []
        if self.do_prologue:
            for fc in range(16):
                cunits.append(self.DV(self.dft_d[2 * fc]))
                cunits.append(self.DV(self.dft_d[2 * fc + 1]))
        for s in range(self.nseq):
            for l in range(self.nlayers):
                base = {0: 0, 1: 48, 2: 92, 3: 140}[l]
                n = 48 if l % 2 == 0 else 44
                for u in range(n):
                    wunits.append(self.DV(self.w_d[base + u]))
                if l % 2 == 0:
                    j = l // 2
                    for fc in range(16):
                        cunits.append(self.DV(self.dft_d[2 * fc]))
                        cunits.append(self.DV(self.dft_d[2 * fc + 1]))
                    for g in range(32):
                        cunits.append(self.DV(self.dft_d[32 + g]))
        self.WR = Ring(self, "wr", "pool", OFF_WR, WR_SLOTS, wunits)
        self.CR = Ring(self, "cr", "sp", OFF_CR, CR_SLOTS, cunits)
        self.pg = 0
        if self.do_prologue:
            self.kf_dft_joint([self.filter_gen(j) for j in range(2)])
            if self.dbg == "kf":
                for j in range(2):
                    for fc in range(16):
                        b = Buf(self.arena, OFF_C + (j * 16 + fc) * 2048, BF16, (1024,))
                        self.dma("sp", b(), V(self.kf_d[j, fc], {("kfd", j, fc)}), P.dma_sem("dbgkf"))
                self.dump("kf", OFF_C)
        for s in range(self.nseq):
            if s == 0 or self.nlayers == 0:
                self.load_x(s)
            for l in range(self.nlayers):
                last = (l == self.nlayers - 1)
                if l % 2 == 0:
                    self.even_mixer(l)
                else:
                    self.odd_mixer(l)
                self.dump(f"z{l}", OFF_X32)
                self.layer_norm(("ln1g", l), ("ln1b", l), final=False)
                self.dump(f"ln1_{l}", OFF_X32)
                self.mlp(l)
                self.dump(f"zm{l}", OFF_X32)
                self.layer_norm(("ln2g", l), ("ln2b", l), final=last)
                self.dump(f"x{l}", OFF_X32)
            if self.nlayers > 0 and s + 1 < self.nseq:
                for o in range(8):
                    self.dma("pool", self.XB(o), self.DV(self.x_d[s + 1, :, o, :]), self.prog.dma_sem(f"xb{o}"))
            self.store_y(s)
            if self.nlayers > 0 and s + 1 < self.nseq:
                for o in range(8):
                    self.dma("sp", self.X32(o), self.DV(self.x_d[s + 1, :, o, :]), self.prog.dma_sem(f"xl{o}"))

    def nextg(self):
        g = self.pg
        self.pg ^= 1
        return g

    def load_x(self, s):
        for o in range(8):
            sem = self.prog.dma_sem(f"xl{o}")
            self.dma("sp", self.X32(o), self.DV(self.x_d[s, :, o, :]), sem)
        for o in range(8):
            self.copy("act" if o % 2 == 0 else "dve", self.XB(o), self.X32(o))

    def store_y(self, s):
        for o in range(8):
            sem = self.prog.dma_sem(f"ys{o}")
            self.dma("sp", V(self.y_d[s, :, o, :], {("ydram", s, o)}), self.X32(o), sem)

    def proj(self, wv, sub, src_chunks, g):
        nk = len(src_chunks)
        for kc in range(nk):
            for tt in range(4):
                self.mm(self.PS(4 * g + tt), wv(sub, kc), src_chunks[kc]((tt * 512, tt * 512 + 512)),
                        start=(kc == 0), stop=(kc == nk - 1))

    def wr_pair_iter(self, nchunks):
        cur = None
        for i in range(nchunks):
            if i % 2 == 0:
                cur = self.WR.acquire()
            slot = self.WR.slot(cur)
            wb = Buf(self.arena, slot.off, BF16, (2, 8, 128))
            yield i, wb, i % 2
            if i % 2 == 1:
                self.WR.release(cur)

    def even_mixer(self, l):
        j = l // 2
        ar = self.arena
        xb = [lambda r, o=o: self.XB(o, r) for o in range(8)]
        ya = Buf(ar, OFF_C2, BF16, (4, T))
        ac = Buf(ar, OFF_TT, BF16, (T,))
        tp = Buf(ar, OFF_C3, F32, (T,))
        cv = Buf(ar, OFF_C3 + 8192, F32, (T,))
        for i, wb, sub in self.wr_pair_iter(12):
            c, which = i // 3, i % 3
            g = self.nextg()
            self.proj(lambda s_, kc, wb=wb: wb(s_, kc), sub, xb, g)
            pg = self.PS(4 * g, 4)
            if which == 0:
                self.act(ac(), pg, AF.Copy)
            elif which == 1:
                self.tt("dve", tp(), pg, ac(), ALU.mult)
                self.ts("dve", cv(), tp(), self.col((("conva", j, 1), c)), ALU.mult)
                self.stt(cv((1, T)), tp((0, T - 1)), self.col((("conva", j, 0), c)), cv((1, T)), ALU.mult, ALU.add)
                self.stt(cv((0, T - 1)), tp((1, T)), self.col((("conva", j, 2), c)), cv((0, T - 1)), ALU.mult, ALU.add)
            else:
                self.tt("dve", ya(c), pg, cv(), ALU.mult)
        self.dump(f"ya{l}", OFF_C)
        x0 = Buf(ar, OFF_C0, BF16, (4, T))
        u = Buf(ar, OFF_C1, BF16, (4, T))
        tbs = [Buf(ar, OFF_C3, F32, (T,)), Buf(ar, OFF_C3 + 8192, F32, (T,))]
        x1c = Buf(ar, OFF_TT, BF16, (T,))
        for i, wb, sub in self.wr_pair_iter(12):
            c, which = i // 3, i % 3
            hc = {0: 4 + c, 1: 8 + c, 2: c}[which]
            g = self.nextg()
            self.proj(lambda s_, kc, wb=wb: wb(s_, kc), sub, xb, g)
            pg = self.PS(4 * g, 4)
            tb = tbs[i % 2]
            w0 = self.col((("shortw", j, 0), hc))
            w1 = self.col((("shortw", j, 1), hc))
            w2 = self.col((("shortw", j, 2), hc))
            bb = self.col((("shortb", j), hc))
            self.act(tb(), pg, AF.Identity, bias=bb, scale=w1)
            self.stt(tb((1, T)), self.PS(4 * g, 4, 0, T - 1), w0, tb((1, T)), ALU.mult, ALU.add)
            if which == 1:
                self.stt(tb((0, T - 1)), self.PS(4 * g, 4, 1, T), w2, tb((0, T - 1)), ALU.mult, ALU.add)
                self.tt("dve", u(c), tb(), x1c(), ALU.mult)
            else:
                dst = (lambda r: x1c(r)) if which == 0 else (lambda r, c=c: x0(c, r))
                self.stt(dst((0, T - 1)), self.PS(4 * g, 4, 1, T), w2, tb((0, T - 1)), ALU.mult, ALU.add)
                self.copy("act", dst((T - 1, T)), tb((T - 1, T)))
        self.dump(f"hy{l}", OFF_C)
        uT = Buf(ar, OFF_XB, BF16, (16, H))
        for tch in range(16):
            g = self.nextg()
            for c in range(4):
                self.transpose(self.PSB(4 * g, c * 128, (c + 1) * 128), u(c, (tch * 128, tch * 128 + 128)), self.IDN())
            self.copy("act" if tch % 2 == 0 else "dve", uT(tch), self.PSB(4 * g, 0, 512))
        Yre = Buf(ar, OFF_C1, BF16, (16, H))
        Yim = Buf(ar, OFF_C3, BF16, (16, H))
        tmp = Buf(ar, OFF_XB + 16384, F32, (6, H))
        kfbuf = [Buf(ar, OFF_XB + 16384 + 12288, BF16, (2, H)), Buf(ar, OFF_XB + 16384 + 12288 + 2048, BF16, (2, H))]
        self.fwd_dft(lambda kc: uT(kc), lambda kc: uT(kc), pointwise=(Yre, Yim, tmp, kfbuf, j))
        self.dump(f"fft{l}", OFF_C)
        yb = Buf(ar, OFF_XB, BF16, (4, T))
        for h in range(2):
            for fc in range(16):
                gi = self.CR.acquire()
                gs = self.CR.slot(gi)
                gb = Buf(ar, gs.off, BF16, (2, 1024))
                for i, Y in enumerate((Yre, Yim)):
                    for c in range(4):
                        for t2 in range(2):
                            self.mm(self.PS(2 * c + t2), Y(fc, (c * 128, c * 128 + 128)), gb(i, (t2 * 512, t2 * 512 + 512)),
                                    start=(fc == 0 and i == 0), stop=(fc == 15 and i == 1))
                self.CR.release(gi)
            for c in range(4):
                self.tt("dve", yb(c, (h * 1024, h * 1024 + 1024)), self.PS(2 * c, 2), x0(c, (h * 1024, h * 1024 + 1024)), ALU.mult)
        self.dump(f"yb{l}", OFF_XB)
        zb = Buf(ar, OFF_C0, BF16, (8, T))
        zq_lo = Buf(ar, OFF_C3, BF16, (4, T))
        zq_hi = Buf(ar, OFF_XB + 16384, BF16, (4, T))
        zbv = [lambda r, o=o: zb(o, r) for o in range(8)]
        zqv = [(lambda r, o=o: zq_lo(o, r)) if o < 4 else (lambda r, o=o: zq_hi(o - 4, r)) for o in range(8)]
        src = [lambda r, o=o: ya(o, r) for o in range(4)] + [lambda r, o=o: yb(o, r) for o in range(4)]
        self.out_proj(src, zbv, zqv, bias_key=None)
        self.ln_layout = (zbv, zqv, OFF_C2)

    def out_proj(self, src, zbv, zqv, bias_key):
        for i, wb, sub in self.wr_pair_iter(8):
            o = i
            g = self.nextg()
            self.proj(lambda s_, kc, wb=wb: wb(s_, kc), sub, src, g)
            pg = self.PS(4 * g, 4)
            self.stt(self.X32(o), self.X32(o), ALPHA, pg, ALU.mult, ALU.add)
            if bias_key is not None:
                self.ts("dve", self.X32(o), self.X32(o), self.col((bias_key, o)), ALU.add)
            self.act(zbv[o]((0, T)), self.X32(o), AF.Copy)
            self.act(zqv[o]((0, T)), self.X32(o), AF.Square)

    def fwd_dft(self, mov_re, mov_im, pointwise=None, kfgen=None):
        ar = self.arena
        def kfload(fc):
            kfbuf, j = pointwise[3], pointwise[4]
            dst = Buf(ar, kfbuf[fc % 2].off, BF16, (2 * H,))
            self.dma("sp", dst(), V(self.kf_d[j, fc], {("kfd", j, fc)}), self.prog.dma_sem(f"kfl{fc % 2}"))

        if pointwise is not None:
            kfload(0)
        for fc in range(16):
            if pointwise is not None:
                if fc + 1 < 16:
                    kfload(fc + 1)
                kfb = pointwise[3][fc % 2]
            g = self.nextg()
            for part, mov in enumerate((mov_re, mov_im)):
                fi = self.CR.acquire()
                fb = Buf(ar, self.CR.slot(fi).off, BF16, (16, 128))
                for kc in range(16):
                    self.mm(self.PS(4 * g + part), fb(kc), mov(kc), start=(kc == 0), stop=(kc == 15))
                self.CR.release(fi)
            pre, pim = self.PS(4 * g), self.PS(4 * g + 1)
            if pointwise is not None:
                Yre, Yim, tmp = pointwise[0], pointwise[1], pointwise[2]
                b = fc % 2
                ure, uim = tmp(b), tmp(2 + b)
                self.act(ure, pre, AF.Copy)
                self.act(uim, pim, AF.Copy)
                self.tt("dve", tmp(4), ure, kfb(0), ALU.mult)
                self.tt("dve", tmp(5), uim, kfb(1), ALU.mult)
                self.tt("dve", Yre(fc), tmp(4), tmp(5), ALU.subtract)
                self.tt("dve", tmp(4), ure, kfb(1), ALU.mult)
                self.tt("dve", tmp(5), uim, kfb(0), ALU.mult)
                self.tt("dve", Yim(fc), tmp(4), tmp(5), ALU.add)
            else:
                kfgen(fc, pre, pim)

    def ln_stats_tt(self, zbv, zqv, mean, rstd, tt):
        r = (tt * 512, tt * 512 + 512)
        for o in range(8):
            self.mm(self.PS(tt), self.ONESB(), zbv[o](r), start=(o == 0), stop=(o == 7))
        for o in range(8):
            self.mm(self.PS(4 + tt), self.ONESB(), zqv[o](r), start=(o == 0), stop=(o == 7))
        self.act(rstd(r), self.PS(tt), AF.Square)
        self.stt(rstd(r), self.PS(4 + tt), EPS, rstd(r), ALU.add, ALU.subtract)
        self.act(rstd(r), rstd(r), AF.Ln)
        self.act(rstd(r), rstd(r), AF.Exp, scale=-0.5)

    def ln_pipeline(self, zbv, zqv, mean, rstd, norm_fn):
        for tt in range(4):
            self.ln_stats_tt(zbv, zqv, mean, rstd, tt)
        for tt in range(4):
            norm_fn(tt)
        self.pg = 0

    @staticmethod
    def bcn(v, n):
        return V(v.ap.unsqueeze(1).broadcast_to([128, n, 512]), v.res)

    def ln_center_scale(self, buf, mean, rstd, r, nd=6):
        xv = buf(None, r)
        self.tt("dve", xv, xv, self.bcn(self.PS(r[0] // 512), 8), ALU.subtract)
        self.tt("dve", xv, xv, self.bcn(rstd(r), 8), ALU.mult)

    def layer_norm(self, gkey, bkey, final):
        zbv, zqv, off_tmp = self.ln_layout
        mean = Buf(self.arena, off_tmp, F32, (T,))
        rstd = Buf(self.arena, off_tmp + 8192, F32, (T,))

        def norm(tt):
            r = (tt * 512, tt * 512 + 512)
            self.ln_center_scale(self.X32, mean, rstd, r)
            for o in range(8):
                g, b = self.col((gkey, o)), self.col((bkey, o))
                if not final:
                    self.act(self.XB(o, r), self.X32(o, r), AF.Identity, bias=b, scale=g)
                if o < 1 and not final:
                    self.ts("dve", self.X32(o, r), self.X32(o, r), g, ALU.mult, b, ALU.add)
                else:
                    self.act(self.X32(o, r), self.X32(o, r), AF.Identity, bias=b, scale=g)

        self.ln_pipeline(zbv, zqv, mean, rstd, norm)

    def mlp(self, l):
        ar = self.arena
        HID = Buf(ar, OFF_C, BF16, (16, T))
        xb = [lambda r, o=o: self.XB(o, r) for o in range(8)]
        zb = Buf(ar, OFF_XB, BF16, (8, T))
        zq = Buf(ar, OFF_C0, BF16, (8, T))
        for half in range(2):
            for i, wb, sub in self.wr_pair_iter(16):
                g = self.nextg()
                self.proj(lambda s_, kc, wb=wb: wb(s_, kc), sub, xb, g)
                pg = self.PS(4 * g, 4)
                self.act(HID(i), pg, AF.Square)
                self.stt(HID(i), pg, 0.0, HID(i), ALU.is_gt, ALU.mult)
            for o in range(8):
                wi = self.WR.acquire()
                wb = Buf(ar, self.WR.slot(wi).off, BF16, (16, 128))
                g = self.nextg()
                for kc in range(16):
                    for tt in range(4):
                        self.mm(self.PS(4 * g + tt), wb(kc), HID(kc, (tt * 512, tt * 512 + 512)), start=(kc == 0), stop=(kc == 15))
                self.WR.release(wi)
                pg = self.PS(4 * g, 4)
                if half == 0:
                    self.stt(self.X32(o), self.X32(o), ALPHA, pg, ALU.mult, ALU.add)
                else:
                    self.tt("dve", self.X32(o), self.X32(o), pg, ALU.add)
                    self.act(zb(o), self.X32(o), AF.Copy)
        for o in range(8):
            if o % 2 == 0:
                self.act(zq(o), self.X32(o), AF.Square)
            else:
                self.tt("dve", zq(o), zb(o), zb(o), ALU.mult)
        zbv = [lambda r, o=o: zb(o, r) for o in range(8)]
        zqv = [lambda r, o=o: zq(o, r) for o in range(8)]
        self.ln_layout = (zbv, zqv, OFF_C2)

    def odd_mixer(self, l):
        j = l // 2
        ar = self.arena
        xb = [lambda r, o=o: self.XB(o, r) for o in range(8)]
        H16 = Buf(ar, OFF_C0, BF16, (8, T))
        diag = [Buf(ar, OFF_C2, BF16, (CONF_K, 128)), Buf(ar, OFF_C2 + 7936, BF16, (CONF_K, 128))]
        sg = Buf(ar, OFF_C2 + 15872, F32, (T,))
        glu = [Buf(ar, OFF_C2 + 24064, BF16, (T + 2 * PADC,)), Buf(ar, OFF_C2 + 28224, BF16, (T + 2 * PADC,))]
        for b in range(2):
            self.memset("dve", glu[b]((0, PADC)), 0.0)
            self.memset("dve", glu[b]((T + PADC, T + 2 * PADC)), 0.0)
        idn = self.IDN()
        idn_bc = V(idn.ap.unsqueeze(1).broadcast_to([128, CONF_K, 128]), idn.res)

        def build_diag(c):
            c0 = self.cols.idx[("dww", j, c, 0)]
            wv = self.KC((c0, c0 + CONF_K))
            w_bc = V(wv.ap.unsqueeze(2).broadcast_to([128, CONF_K, 128]), wv.res)
            self.tt("dve", diag[c % 2](), idn_bc, w_bc, ALU.mult)

        def conv(c):
            gl, dg = glu[c % 2], diag[c % 2]
            g2 = self.nextg()
            for k in range(CONF_K):
                for tt in range(4):
                    self.mm(self.PS(4 * g2 + tt), dg(k), gl((tt * 512 + k, tt * 512 + k + 512)),
                            start=(k == 0), stop=(k == CONF_K - 1))
            self.act(H16(c), self.PS(4 * g2, 4), AF.Identity, bias=self.col((("dwb", j), c)))

        build_diag(0)
        it = self.wr_pair_iter(16)
        for step in range(16):
            i, wb, sub = next(it)
            c, which = i // 2, i % 2
            g = self.nextg()
            self.proj(lambda s_, kc, wb=wb: wb(s_, kc), sub, xb, g)
            pg = self.PS(4 * g, 4)
            if which == 0:
                self.act(sg(), pg, AF.Sigmoid, bias=self.col((("bpw1", j), 8 + c)))
                if c >= 1:
                    conv(c - 1)
            else:
                self.stt(glu[c % 2]((PADC, T + PADC)), pg, self.col((("bpw1", j), c)), sg(), ALU.add, ALU.mult)
                if c + 1 < 8:
                    build_diag(c + 1)
        for _ in it:
            pass
        conv(7)
        hq = Buf(ar, OFF_XB, BF16, (8, T))
        for o in range(8):
            self.tt("dve", hq(o), H16(o), H16(o), ALU.mult)
        mean = Buf(ar, OFF_C2, F32, (T,))
        rstd = Buf(ar, OFF_C2 + 8192, F32, (T,))
        hbv = [lambda r, o=o: H16(o, r) for o in range(8)]
        hqv = [lambda r, o=o: hq(o, r) for o in range(8)]
        def cnorm(tt):
            r = (tt * 512, tt * 512 + 512)
            self.ln_center_scale(H16, mean, rstd, r, nd=5)
            for o in range(8):
                self.act(H16(o, r), H16(o, r), AF.Silu, scale=self.col((("olng", j), o)), bias=self.col((("olnb", j), o)))

        self.ln_pipeline(hbv, hqv, mean, rstd, cnorm)
        zb = Buf(ar, OFF_XB, BF16, (8, T))
        zq = Buf(ar, OFF_C2, BF16, (8, T))
        zbv = [lambda r, o=o: zb(o, r) for o in range(8)]
        zqv = [lambda r, o=o: zq(o, r) for o in range(8)]
        self.out_proj(hbv, zbv, zqv, bias_key=("bpw2", j))
        self.ln_layout = (zbv, zqv, OFF_C0)

    def filter_gen(self, j):
        ar = self.arena
        P = self.prog
        p64 = (0, 64)
        o = OFF_C
        zT = Buf(ar, o, F32, (T,)); o += 8192
        hA = Buf(ar, o, F32, (T,)); o += 8192
        hB = Buf(ar, o, F32, (T,)); o += 8192
        arg = Buf(ar, o, F32, (T,)); o += 8192
        msk = Buf(ar, o, F32, (T,)); o += 8192
        w4 = Buf(ar, o, F32, (1024,)); o += 4096
        w1 = Buf(ar, o, F32, (64,)); o += 256
        w2 = Buf(ar, o, F32, (64,)); o += 256
        w3 = Buf(ar, o, F32, (64,)); o += 256
        hyb = Buf(ar, o, F32, (H,)); o += 2048
        rn = Buf(ar, o, F32, (H,)); o += 2048
        kf = Buf(ar, o, F32, (H,)); o += 2048
        kb = Buf(ar, o, F32, (H,)); o += 2048
        akf = Buf(ar, o, F32, (H,)); o += 2048
        akb = Buf(ar, o, F32, (H,)); o += 2048
        A = Buf(ar, o, F32, (H,)); o += 2048
        t1 = Buf(ar, o, F32, (H,)); o += 2048
        assert o <= OFF_C + 65536
        if j == 0:
            S = Buf(ar, OFF_XB, BF16, (16, H))
            Dd = Buf(ar, OFF_XB + 16384, BF16, (16, H))
        else:
            S = Buf(ar, OFF_X32 + 40960, BF16, (16, H))
            Dd = Buf(ar, OFF_C + 24576, BF16, (16, H))
            hyb = Buf(ar, OFF_X32 + 57344, F32, (H,))
            rn = Buf(ar, OFF_X32 + 59392, F32, (H,))
        dec = Buf(ar, OFF_X32, F32, (16, H))
        stg = [Buf(ar, OFF_X32 + 32768 + (2 * j + b) * 2048, BF16, (2 * H,)) for b in range(2)]
        fsem = P.dma_sem("flt")
        if j == 0:
            self.dma("sp", zT(parts=(0, 33)), self.DV(self.zT_d), fsem)
            self.dma("sp", dec(), self.DV(self.dec_d), fsem)
        self.dma("sp", w1(parts=(0, 33)), self.DV(self.fw1_d[j]), P.dma_sem("flt1"))
        self.dma("sp", w2(parts=p64), self.DV(self.fw2_d[j]), P.dma_sem("flt2"))
        self.dma("sp", w3(parts=p64), self.DV(self.fw3_d[j]), P.dma_sem("flt3"))
        self.dma("sp", w4(parts=p64), self.DV(self.fw4_d[j]), P.dma_sem("flt4"))
        self.dma("sp", hyb(), self.DV(self.hyb_d[j].partition_broadcast(128)), P.dma_sem("flt5"))
        fr = self.col(("fr", j), parts=p64)
        PI = 3.14159
        TWO_PI = 2.0 * math.pi
        layers = [(w1, (0, 33), zT, "fb1", "frb1", hA), (w2, p64, hA, "fb2", "frb2", hB), (w3, p64, hB, "fb3", "frb3", hA)]
        for (w, kp, src, bn, fbn, dst) in layers:
            frb = self.col((fbn, j), parts=p64)
            self.tt("dve", frb, fr, self.col((bn, j), parts=p64), ALU.mult)
            g = self.nextg()
            for tt in range(4):
                self.mm(self.PS(4 * g + tt, parts=p64), w(parts=kp), src((tt * 512, tt * 512 + 512), parts=kp), start=True, stop=True)
            self.act(arg(parts=p64), self.PS(4 * g, 4, parts=p64), AF.Identity, bias=frb, scale=fr)
            for _ in range(2):
                self.ts("dve", msk(parts=p64), arg(parts=p64), PI, ALU.is_gt, -TWO_PI, ALU.mult)
                self.tt("dve", arg(parts=p64), arg(parts=p64), msk(parts=p64), ALU.add)
                self.ts("dve", msk(parts=p64), arg(parts=p64), -PI, ALU.is_lt, TWO_PI, ALU.mult)
                self.tt("dve", arg(parts=p64), arg(parts=p64), msk(parts=p64), ALU.add)
            self.act(dst(parts=p64), arg(parts=p64), AF.Sin)
        h3 = hA
        NB = 2
        kf2 = [kf, Buf(ar, OFF_X32 + 61440, F32, (H,))]
        kb2 = [kb, Buf(ar, OFF_X32 + 63488, F32, (H,))]
        akf2 = [akf, Buf(ar, OFF_TT, F32, (H,))]
        akb2 = [akb, Buf(ar, OFF_TT + 2048, F32, (H,))]
        for nch in range(16):
            b = nch % 2
            kf_, kb_, akf_, akb_ = kf2[b], kb2[b], akf2[b], akb2[b]
            g = self.nextg()
            lh = h3((nch * 128, nch * 128 + 128), parts=p64)
            self.mm(self.PS(4 * g), lh, w4((0, 512), parts=p64), start=True, stop=True)
            self.mm(self.PS(4 * g + 1), lh, w4((512, 1024), parts=p64), start=True, stop=True)
            self.tt("dve", kf_(), self.PS(4 * g), dec(nch), ALU.mult)
            self.tt("dve", kb_(), self.PS(4 * g + 1), dec(nch), ALU.mult)
            if nch == 0:
                self.memset("dve", kb_(parts=(0, 1)), 0.0)
            self.act(akf_(), kf_(), AF.Abs)
            self.act(akb_(), kb_(), AF.Abs)
            self.tt("dve", S(nch), kf_(), kb_(), ALU.add)
            self.tt("dve", Dd(nch), kf_(), kb_(), ALU.subtract)
            self.mm(self.PS(NB), self.ONESF(), akf_(), start=(nch == 0), stop=False)
            self.mm(self.PS(NB), self.ONESF(), akb_(), start=False, stop=(nch == 15))
        self.recip(rn(), self.PS(NB))

        def kfgen(fc, pre, pim):
            st_ = stg[fc % 2]
            self.tt("dve", t1(), pre, rn(), ALU.mult)
            self.tt("dve", st_((0, H)), t1(), hyb(), ALU.add)
            self.tt("dve", st_((H, 2 * H)), pim, rn(), ALU.mult)
            self.dma("sp", V(self.kf_d[j, fc], {("kfd", j, fc)}), st_(), P.dma_sem(f"kfw{j}{fc % 2}"))

        return {"S": S, "D": Dd, "kfgen": kfgen}

    def kf_dft_joint(self, lays):
        ar = self.arena
        for fc in range(16):
            g = self.nextg()
            for part in range(2):
                fi = self.CR.acquire()
                fb = Buf(ar, self.CR.slot(fi).off, BF16, (16, 128))
                for li, lay in enumerate(lays):
                    mov = lay["S"] if part == 0 else lay["D"]
                    for kc in range(16):
                        self.mm(self.PS(4 * g + 2 * li + part), fb(kc), mov(kc), start=(kc == 0), stop=(kc == 15))
                self.CR.release(fi)
            for li, lay in enumerate(lays):
                lay["kfgen"](fc, self.PS(4 * g + 2 * li), self.PS(4 * g + 2 * li + 1))


def host_inputs(inp):
    cols = build_cols(inp)
    zT, dec = filter_consts()
    common = {
        "wts": pack_weights(inp),
        "dft": dft_tables(),
        "cols": cols.arr(),
        "ident": np.eye(128, dtype=np.float32).astype(ml_dtypes.bfloat16),
        "zT": zT,
        "dec": dec,
        "fw1": np.ascontiguousarray(inp["e_flt_w1"], np.float32),
        "fw2": np.ascontiguousarray(inp["e_flt_w2"], np.float32),
        "fw3": np.ascontiguousarray(inp["e_flt_w3"], np.float32),
        "fw4": np.ascontiguousarray(inp["e_flt_w4"], np.float32),
        "hyb": np.ascontiguousarray(inp["e_hy_bias"], np.float32).reshape(2, 1, H),
    }
    return cols, common


def to_dev_layout(x):
    n = x.shape[0]
    return np.ascontiguousarray(x.reshape(n, T, 8, 128).transpose(0, 3, 2, 1))


def from_dev_layout(y):
    n = y.shape[0]
    return np.ascontiguousarray(y.transpose(0, 3, 2, 1)).reshape(n, T, D)


def kernel(**inp):
    inp = {k: np.asarray(v) for k, v in inp.items()}
    cols, common = host_inputs(inp)
    nseq = NSEQ_TOTAL // NCORES
    kb = KB(nseq)
    nc = kb.build(cols)
    xs = np.concatenate([inp["x_prompt"], inp["x_sample"]], axis=0).astype(np.float32)
    in_maps = []
    for c in range(NCORES):
        m = dict(common)
        m["x"] = to_dev_layout(xs[c * nseq:(c + 1) * nseq])
        in_maps.append(m)
    res = run_bass_kernel_spmd(nc, in_maps, core_ids=list(range(NCORES)))
    ys = np.concatenate([from_dev_layout(np.asarray(res.results[c]["y"], np.float32)) for c in range(NCORES)], axis=0)
    nb = inp["x_prompt"].shape[0]
    return ys[:nb], ys[nb:]
```

```python
import math
import numpy as np
import ml_dtypes
import concourse.bass as bass
import concourse.mybir as mybir
from concourse.bass_utils import run_bass_kernel_spmd
from contextlib import ExitStack

F32 = mybir.dt.float32
BF16 = mybir.dt.bfloat16
AF = mybir.ActivationFunctionType
ALU = mybir.AluOpType

D = 1024
T = 2048
H = 512
DFF = 4096
NCORES = 8
NSEQ_TOTAL = 48
ALPHA = float(8 ** 0.25)
EPS = 1e-5
NFFT = 4096
CONF_K = 31
PADC = 15

ENGS = ["pe", "act", "dve", "pool", "sp"]

OFF_X32 = 0
OFF_XB = 65536
OFF_C = 98304
OFF_C0, OFF_C1, OFF_C2, OFF_C3 = OFF_C, OFF_C + 16384, OFF_C + 32768, OFF_C + 49152
WR_SLOTS = 5
CR_SLOTS = 4
OFF_WR = 163840
OFF_CR = OFF_WR + WR_SLOTS * 4096
OFF_TT = OFF_CR + CR_SLOTS * 4096
TT_BYTES = 5120
OFF_K = OFF_TT + TT_BYTES
NCOL = 880
OFF_ID = OFF_K + NCOL * 4
OFF_ONESB = OFF_ID + 256
OFF_ONESF = OFF_ONESB + 256
OFF_I4 = OFF_ONESF + 512
OFF_NEGF = OFF_I4 + 64
ARENA_BYTES = OFF_NEGF + 512
assert ARENA_BYTES <= 210944, ARENA_BYTES


class V:
    __slots__ = ("ap", "res")

    def __init__(self, ap, res):
        self.ap = ap
        self.res = frozenset(res)


def _prod(s):
    r = 1
    for x in s:
        r *= x
    return r


class Buf:
    def __init__(self, arena, off, dtype, shape):
        self.esz = 4 if dtype == F32 else 2
        self.off = off
        self.shape = tuple(shape)
        n = _prod(shape)
        assert off % 4 == 0 or self.esz == 2
        a = arena[:, off // 2: off // 2 + n * self.esz // 2]
        if dtype == F32:
            a = a.bitcast(F32)
        if len(shape) == 2:
            a = a.rearrange("p (a b) -> p a b", a=shape[0])
        elif len(shape) == 3:
            a = a.rearrange("p (a b c) -> p a b c", a=shape[0], b=shape[1])
        self.a = a
        st = []
        s = self.esz
        for d in reversed(shape):
            st.append(s)
            s *= d
        self.strides = tuple(reversed(st))
        self.nbytes = n * self.esz

    def __call__(self, *idx, parts=None):
        idx = list(idx) + [None] * (len(self.shape) - len(idx))
        key = [slice(None) if parts is None else slice(parts[0], parts[1])]
        rngs = []
        for d, ix in enumerate(idx):
            if ix is None:
                key.append(slice(None))
                rngs.append((0, self.shape[d]))
            elif isinstance(ix, tuple):
                key.append(slice(ix[0], ix[1]))
                rngs.append((ix[0], ix[1]))
            else:
                key.append(ix)
                rngs.append((ix, ix + 1))
        ap = self.a[tuple(key)]
        res = set()
        outer = rngs[:-1]
        lo, hi = rngs[-1]
        quads = range(4) if parts is None else range(parts[0] // 32, (parts[1] - 1) // 32 + 1)

        def rec(d, base):
            if d == len(outer):
                b0 = base + lo * self.strides[-1]
                b1 = base + hi * self.strides[-1] - 1
                for blk in range(b0 // 1024, b1 // 1024 + 1):
                    for qd in quads:
                        res.add(("sb", blk, qd))
                return
            for i in range(outer[d][0], outer[d][1]):
                rec(d + 1, base + i * self.strides[d])

        rec(0, self.off)
        return V(ap, res)


class Prog:
    def __init__(self):
        self.ins = []
        self.cnt = {e: 0 for e in ENGS}
        self.lastw = {}
        self.rds = {}
        self.keyidx = {("e", e): i for i, e in enumerate(ENGS)}
        self.dcnt = {}
        self.last_vc = {e: None for e in ENGS}
        self.tokvc = {}
        self.nk = 72

    def dma_sem(self, name):
        if name not in self.dcnt:
            self.dcnt[name] = 0
            self.keyidx[("d", name)] = len(self.keyidx)
            assert len(self.keyidx) <= self.nk
        return name

    def add(self, eng, fn, reads=(), writes=(), dsem=None):
        rres = set()
        for v in reads:
            rres |= v.res
        wres = set()
        for v in writes:
            wres |= v.res
        deps = {}
        for r in rres:
            t = self.lastw.get(r)
            if t is not None:
                deps[t] = True
        for w in wres:
            t = self.lastw.get(w)
            if t is not None and t not in deps:
                deps[t] = False
            rd = self.rds.get(w)
            if rd:
                for t in rd.values():
                    if t not in deps:
                        deps[t] = False
        self.cnt[eng] += 1
        n = self.cnt[eng]
        fl = []
        for t, raw in deps.items():
            if t[0] == "e" and t[1] == eng:
                if eng == "pe" or not raw:
                    continue
            fl.append(t)
        vc = np.zeros(self.nk, np.int64) if self.last_vc[eng] is None else self.last_vc[eng].copy()
        for t in fl:
            np.maximum(vc, self.tokvc[t], out=vc)
        mytok = ("e", eng, n)
        vc_issue = vc
        vc_e = vc.copy()
        vc_e[self.keyidx[("e", eng)]] = n
        self.last_vc[eng] = vc_issue
        self.tokvc[mytok] = vc_e
        if dsem is not None:
            self.dcnt[dsem] += 1
            tok = ("d", dsem, 16 * self.dcnt[dsem])
            vcd = vc.copy()
            vcd[self.keyidx[("d", dsem)]] = tok[2]
            self.tokvc[tok] = vcd
        else:
            tok = mytok
            self.last_vc[eng] = vc_issue
        for r in rres:
            d = self.rds.setdefault(r, {})
            k = (tok[0], tok[1]) if tok[0] == "e" else tok
            d[k] = tok
        for w in wres:
            self.lastw[w] = tok
            self.rds[w] = {}
        self.ins.append({"eng": eng, "fn": fn, "deps": fl, "tok": mytok, "dsem": dsem})
        return tok

    def emit(self, nc, block, esems, dsems):
        sig = {e: set() for e in ENGS}
        for I in self.ins:
            for t in I["deps"]:
                if t[0] == "e":
                    sig[t[1]].add(t[2])
        rank = {}
        for e in ENGS:
            r = {}
            for i, n in enumerate(sorted(sig[e])):
                r[n] = i + 1
            rank[e] = r
        per = {e: [] for e in ENGS}
        for I in self.ins:
            per[I["eng"]].append(I)
        keyidx = self.keyidx
        tokvc = self.tokvc
        stats = {"waits": 0}

        def run(e, engobj):
            known = np.zeros(self.nk, np.int64)
            for I in per[e]:
                deps = sorted(I["deps"], key=lambda t: -int(tokvc[t].sum()))
                for t in deps:
                    ki = keyidx[(t[0], t[1])]
                    if known[ki] >= t[2]:
                        continue
                    if t[0] == "e":
                        engobj.wait_ge(esems[t[1]], rank[t[1]][t[2]])
                    else:
                        engobj.wait_ge(dsems[t[1]], t[2])
                    stats["waits"] += 1
                    np.maximum(known, tokvc[t], out=known)
                if I["fn"] is None:
                    continue
                inst = I["fn"](engobj)
                if I["dsem"] is not None:
                    inst.then_inc(dsems[I["dsem"]], 16)
                elif I["tok"][2] in sig[e]:
                    inst.then_inc(esems[e], 1)

        @block.tensor
        def _(t):
            run("pe", t)

        @block.scalar
        def _(s):
            run("act", s)

        @block.vector
        def _(v):
            run("dve", v)

        @block.gpsimd
        def _(g):
            run("pool", g)

        @block.sync
        def _(s):
            run("sp", s)

        return stats


def _pair_units(W, chunks):
    K = W.shape[0]
    assert K == 1024 and len(chunks) % 2 == 0
    Wr = W.reshape(8, 128, -1, 128)
    sel = Wr[:, :, chunks, :]
    sel = sel.transpose(2, 1, 0, 3)
    n = len(chunks)
    sel = sel.reshape(n // 2, 2, 128, 8, 128).transpose(0, 2, 1, 3, 4)
    return np.ascontiguousarray(sel).reshape(n // 2, 128, 2048)


def _w2_units(W2, half):
    Wh = W2[half * 2048:(half + 1) * 2048].reshape(16, 128, 8, 128)
    return np.ascontiguousarray(Wh.transpose(2, 1, 0, 3)).reshape(8, 128, 2048)


HY_ORDER = []
for _c in range(4):
    HY_ORDER += [16 + _c, 20 + _c, 12 + _c]
MA_ORDER = []
for _c in range(4):
    MA_ORDER += [4 + _c, 8 + _c, 0 + _c]
PW1_ORDER = []
for _c in range(8):
    PW1_ORDER += [8 + _c, _c]


def pack_weights(inp):
    units = []
    for l in range(4):
        j = l // 2
        if l % 2 == 0:
            w_in = inp["e_w_in"][j]
            units.append(_pair_units(w_in, MA_ORDER))
            units.append(_pair_units(w_in, HY_ORDER))
            units.append(_pair_units(inp["e_w_out"][j], list(range(8))))
        else:
            units.append(_pair_units(inp["o_w_pw1"][j], PW1_ORDER))
            units.append(_pair_units(inp["o_w_pw2"][j], list(range(8))))
        w1 = inp["mlp_w1"][l]
        w2 = inp["mlp_w2"][l]
        for half in range(2):
            units.append(_pair_units(w1, list(range(half * 16, half * 16 + 16))))
            units.append(_w2_units(w2, half))
    return np.concatenate(units, axis=0).astype(np.float32)


def dft_tables():
    n = np.arange(T, dtype=np.float64)
    f = np.arange(T, dtype=np.float64)
    th = 2.0 * np.pi * (f + 0.5) / NFFT
    ang = np.outer(n, th)
    Fre = np.cos(ang)
    Fim = -np.sin(ang)
    units = []
    for fc in range(16):
        for M in (Fre, Fim):
            blk = M[:, fc * 128:(fc + 1) * 128].reshape(16, 128, 128).transpose(1, 0, 2)
            units.append(blk.reshape(128, 2048))
    s = 2.0 / NFFT
    Gre = (s * Fre).T
    Gim = (s * Fim).T
    for h in range(2):
        for fc in range(16):
            u = np.stack([Gre[fc * 128:(fc + 1) * 128, h * 1024:(h + 1) * 1024],
                          Gim[fc * 128:(fc + 1) * 128, h * 1024:(h + 1) * 1024]], axis=1)
            units.append(u.reshape(128, 2048))
    return np.stack(units).astype(ml_dtypes.bfloat16)


def filter_consts():
    L = T
    t = np.linspace(0.0, 1.0, L, dtype=np.float32)[:, None]
    w = (np.float32(2.0 * math.pi / L)) * np.arange(L, dtype=np.float32)[:, None]
    bands = np.linspace(1e-4, 15, 16, dtype=np.float32)[None, :]
    z = np.concatenate([t, np.cos(bands * w), -np.sin(bands * w)], axis=-1).astype(np.float32)
    max_decay = math.log(1e-2) / 0.3
    min_decay = math.log(1e-2) / 1.5
    deltas = np.abs(np.linspace(min_decay, max_decay, H, dtype=np.float32))
    decay = np.exp(-t * deltas[None, :]).astype(np.float32)
    zT = np.ascontiguousarray(z.T)
    dec = np.ascontiguousarray(decay.reshape(16, 128, H).transpose(1, 0, 2))
    return zT, dec


class Cols:
    def __init__(self):
        self.n = 0
        self.idx = {}
        self.data = []

    def add(self, key, vec):
        self.idx[key] = self.n
        self.n += 1
        self.data.append(np.asarray(vec, np.float32).reshape(128))

    def arr(self):
        a = np.zeros((128, NCOL), np.float32)
        assert self.n <= NCOL, self.n
        a[:, :self.n] = np.stack(self.data, axis=1)
        return a


def build_cols(inp):
    c = Cols()

    def chunks(name, vec, n):
        v = np.asarray(vec, np.float32).reshape(n, 128)
        for o in range(n):
            c.add((name, o), v[o])

    for l in range(4):
        chunks(("ln1g", l), inp["ln1_g"][l], 8)
        chunks(("ln1b", l), inp["ln1_b"][l], 8)
        chunks(("ln2g", l), inp["ln2_g"][l], 8)
        chunks(("ln2b", l), inp["ln2_b"][l], 8)
    for j in range(2):
        for k in range(3):
            chunks(("conva", j, k), inp["e_conv_a"][j][k], 4)
            chunks(("shortw", j, k), inp["e_short_w"][j][k], 12)
        chunks(("shortb", j), inp["e_short_b"][j], 12)
        chunks(("bpw1", j), inp["o_b_pw1"][j], 16)
        chunks(("dwb", j), inp["o_dw_b"][j], 8)
        chunks(("olng", j), inp["o_ln_g"][j], 8)
        chunks(("olnb", j), inp["o_ln_b"][j], 8)
        chunks(("bpw2", j), inp["o_b_pw2"][j], 8)
        dw = np.zeros((32, D), np.float32)
        dw[:CONF_K] = np.asarray(inp["o_dw_w"][j], np.float32)
        for o in range(8):
            for jg in range(4):
                for q in range(8):
                    blk = dw[4 * q:4 * q + 4, o * 128 + 32 * jg: o * 128 + 32 * jg + 32]
                    c.add(("dwq", j, o, jg, q), blk.reshape(128))
        for nm, key in (("fr", "e_flt_freq"), ("fb1", "e_flt_b1"), ("fb2", "e_flt_b2"), ("fb3", "e_flt_b3")):
            v = np.zeros(128, np.float32)
            v[:64] = inp[key][j]
            c.add((nm, j), v)
        for nm in ("frb1", "frb2", "frb3"):
            c.add((nm, j), np.zeros(128, np.float32))
    c.add(("neghalf",), np.full(128, -0.5, np.float32))
    return c


class Ring:
    def __init__(self, kb, name, eng, off, nslots, units):
        self.kb = kb
        self.name = name
        self.eng = eng
        self.n = nslots
        self.units = units
        self.slots = [Buf(kb.arena, off + i * 4096, BF16, (2048,)) for i in range(nslots)]
        self.sems = [kb.prog.dma_sem(f"{name}{i}") for i in range(nslots)]
        self.next_load = 0
        self.next_use = 0

    def _load(self, idx):
        if idx >= len(self.units):
            return
        s = idx % self.n
        src = self.units[idx]
        n = src.ap.shape[-1]
        dst = self.slots[s]((0, n))
        self.kb.dma(self.eng, dst, src, self.sems[s])

    def acquire(self):
        if self.next_use == 0:
            for i in range(self.n):
                self._load(i)
            self.next_load = self.n
        idx = self.next_use
        self.next_use += 1
        return idx

    def slot(self, idx):
        return self.slots[idx % self.n]

    def release(self, idx):
        self._load(idx + self.n)


class KB:
    def __init__(self, nseq, nlayers=4, do_prologue=True, dbg=None):
        self.nseq = nseq
        self.nlayers = nlayers
        self.dbg = dbg
        self.do_prologue = do_prologue
        self.prog = Prog()
        self.nc = bass.Bass("TRN2", target_bir_lowering=False)
        nc = self.nc
        self.x_d = nc.dram_tensor("x", [nseq, 128, 8, T], F32, kind="ExternalInput").ap()
        self.y_d = nc.dram_tensor("y", [nseq, 128, 8, T], F32, kind="ExternalOutput").ap()
        self.w_d = nc.dram_tensor("wts", [184, 128, 2048], F32, kind="ExternalInput").ap()
        self.dft_d = nc.dram_tensor("dft", [64, 128, 2048], BF16, kind="ExternalInput").ap()
        self.cols_d = nc.dram_tensor("cols", [128, NCOL], F32, kind="ExternalInput").ap()
        self.ident_d = nc.dram_tensor("ident", [128, 128], BF16, kind="ExternalInput").ap()
        self.i4_d = nc.dram_tensor("i4", [128, 32], BF16, kind="ExternalInput").ap()
        self.zT_d = nc.dram_tensor("zT", [33, T], F32, kind="ExternalInput").ap()
        self.dec_d = nc.dram_tensor("dec", [128, 16, H], F32, kind="ExternalInput").ap()
        self.fw1_d = nc.dram_tensor("fw1", [2, 33, 64], F32, kind="ExternalInput").ap()
        self.fw2_d = nc.dram_tensor("fw2", [2, 64, 64], F32, kind="ExternalInput").ap()
        self.fw3_d = nc.dram_tensor("fw3", [2, 64, 64], F32, kind="ExternalInput").ap()
        self.fw4_d = nc.dram_tensor("fw4", [2, 64, 1024], F32, kind="ExternalInput").ap()
        self.hyb_d = nc.dram_tensor("hyb", [2, 1, H], F32, kind="ExternalInput").ap()
        self.kf_d = nc.dram_tensor("kfs", [2, 16, 128, 1024], BF16, kind="Internal").ap()

    def mm(self, out, lhsT, rhs, start, stop, tile_position=None):
        if tile_position is None:
            fn = lambda e, o=out.ap, l=lhsT.ap, r=rhs.ap, s=start, p=stop: e.matmul(o, lhsT=l, rhs=r, start=s, stop=p)
        else:
            fn = lambda e, o=out.ap, l=lhsT.ap, r=rhs.ap, s=start, p=stop, tp=tile_position: e.matmul(
                o, lhsT=l, rhs=r, start=s, stop=p, tile_position=tp)
        self.prog.add("pe", fn, reads=[lhsT, rhs], writes=[out])

    def transpose(self, out, in_, ident):
        self.prog.add("pe", lambda e, o=out.ap, i=in_.ap, d=ident.ap: e.transpose(out=o, in_=i, identity=d),
                      reads=[in_, ident], writes=[out])

    def act(self, out, in_, func, bias=None, scale=None, extra_reads=()):
        kw = {}
        rd = [in_] + list(extra_reads)
        if bias is not None:
            if isinstance(bias, V):
                kw["bias"] = bias.ap
                rd.append(bias)
            else:
                kw["bias"] = bias
        if scale is not None:
            if isinstance(scale, V):
                kw["scale"] = scale.ap
                rd.append(scale)
            else:
                kw["scale"] = scale
        self.prog.add("act", lambda e, o=out.ap, i=in_.ap, f=func, kw=kw: e.activation(out=o, in_=i, func=f, **kw),
                      reads=rd, writes=[out])

    def tt(self, eng, out, in0, in1, op):
        self.prog.add(eng, lambda e, o=out.ap, a=in0.ap, b=in1.ap, op=op: e.tensor_tensor(out=o, in0=a, in1=b, op=op),
                      reads=[in0, in1], writes=[out])

    def ts(self, eng, out, in0, s1, op0, s2=None, op1=None):
        rd = [in0]
        a1 = s1
        a2 = s2
        if isinstance(s1, V):
            rd.append(s1)
            a1 = s1.ap
        if isinstance(s2, V):
            rd.append(s2)
            a2 = s2.ap
        if op1 is None:
            fn = lambda e, o=out.ap, a=in0.ap: e.tensor_scalar(out=o, in0=a, scalar1=a1, scalar2=None, op0=op0)
        else:
            fn = lambda e, o=out.ap, a=in0.ap: e.tensor_scalar(out=o, in0=a, scalar1=a1, scalar2=a2, op0=op0, op1=op1)
        self.prog.add(eng, fn, reads=rd, writes=[out])

    def stt(self, out, in0, scalar, in1, op0, op1):
        rd = [in0, in1]
        sc = scalar
        if isinstance(scalar, V):
            rd.append(scalar)
            sc = scalar.ap
        self.prog.add("dve", lambda e, o=out.ap, a=in0.ap, b=in1.ap: e.scalar_tensor_tensor(
            out=o, in0=a, scalar=sc, in1=b, op0=op0, op1=op1), reads=rd, writes=[out])

    def copy(self, eng, out, in_):
        if eng == "act":
            self.act(out, in_, AF.Copy)
        else:
            self.prog.add(eng, lambda e, o=out.ap, i=in_.ap: e.tensor_copy(out=o, in_=i), reads=[in_], writes=[out])

    def memset(self, eng, out, val):
        self.prog.add(eng, lambda e, o=out.ap: e.memset(o, val), reads=[], writes=[out])

    def recip(self, out, in_):
        self.prog.add("dve", lambda e, o=out.ap, i=in_.ap: e.reciprocal(out=o, in_=i), reads=[in_], writes=[out])

    def dma(self, eng, out, in_, sem):
        return self.prog.add(eng, lambda e, o=out.ap, i=in_.ap: e.dma_start(out=o, in_=i), reads=[in_], writes=[out], dsem=sem)

    def PS(self, b, nb=1, lo=0, hi=None, parts=None):
        hi = nb * 512 if hi is None else hi
        ap = self.ps[:, b * 512 + lo: b * 512 + hi] if parts is None else self.ps[parts[0]:parts[1], b * 512 + lo: b * 512 + hi]
        return V(ap, {("ps", bb) for bb in range(b + lo // 512, b + (hi - 1) // 512 + 1)})

    def PSB(self, b, lo, hi):
        ap = self.ps[:, b * 512:(b + 1) * 512].bitcast(BF16)[:, lo:hi]
        return V(ap, {("ps", b)})

    def col(self, key, parts=None):
        i = self.cols.idx[key]
        return self.KC((i, i + 1), parts=parts)

    def DV(self, ap, res=()):
        return V(ap, set(res))

    def build(self, cols):
        self.cols = cols
        nc = self.nc
        with ExitStack() as st:
            self.arena = st.enter_context(nc.sbuf_tensor("arena", [128, ARENA_BYTES // 2], BF16))
            self.ps = st.enter_context(nc.psum_tensor("ps", [128, 4096], F32))
            ar = self.arena
            self.X32 = Buf(ar, OFF_X32, F32, (8, T))
            self.XB = Buf(ar, OFF_XB, BF16, (8, T))
            self.KC = Buf(ar, OFF_K, F32, (NCOL,))
            self.IDN = Buf(ar, OFF_ID, BF16, (128,))
            self.ONESB = Buf(ar, OFF_ONESB, BF16, (128,))
            self.ONESF = Buf(ar, OFF_ONESF, F32, (128,))
            self.I4 = Buf(ar, OFF_I4, BF16, (32,))
            self.NEGF = Buf(ar, OFF_NEGF, F32, (128,))
            self.record()
            esems = {e: st.enter_context(nc.semaphore(f"e_{e}")) for e in ENGS}
            dsems = {n: st.enter_context(nc.semaphore(f"d_{n}")) for n in self.prog.dcnt}
            block = st.enter_context(nc.Block())
            self.stats = self.prog.emit(nc, block, esems, dsems)
        return nc

    def dump(self, stage, off):
        if self.dbg != stage:
            return
        b = Buf(self.arena, off, F32, (8, T))
        for o in range(8):
            self.dma("sp", V(self.y_d[0, :, o, :], {("ydram", 0, o)}), b(o), self.prog.dma_sem(f"ys{o}"))
        raise StopIteration

    def record(self):
        try:
            self.record_()
        except StopIteration:
            pass
        self.prog.add("sp", None, reads=[V(None, {("ydram", s, o) for s in range(self.nseq) for o in range(8)})])

    def record_(self):
        P = self.prog
        csem = P.dma_sem("const")
        self.dma("sp", self.KC(), self.DV(self.cols_d), csem)
        self.dma("sp", self.IDN(), self.DV(self.ident_d), csem)
        self.dma("sp", self.I4(), self.DV(self.i4_d), csem)
        self.memset("dve", self.ONESB(), 1.0 / 1024.0)
        self.memset("dve", self.ONESF(), 1.0)
        self.memset("dve", self.NEGF(), -1.0 / 128.0)
        wunits = []
        cunits = []
        if self.do_prologue:
            for fc in range(16):
                cunits.append(self.DV(self.dft_d[2 * fc]))
                cunits.append(self.DV(self.dft_d[2 * fc + 1]))
        for s in range(self.nseq):
            for l in range(self.nlayers):
                base = {0: 0, 1: 48, 2: 92, 3: 140}[l]
                n = 48 if l % 2 == 0 else 44
                for u in range(n):
                    wunits.append(self.DV(self.w_d[base + u]))
                if l % 2 == 0:
                    j = l // 2
                    for fc in range(16):
                        cunits.append(self.DV(self.dft_d[2 * fc]))
                        cunits.append(self.DV(self.dft_d[2 * fc + 1]))
                    for g in range(32):
                        cunits.append(self.DV(self.dft_d[32 + g]))
        self.WR = Ring(self, "wr", "pool", OFF_WR, WR_SLOTS, wunits)
        self.CR = Ring(self, "cr", "sp", OFF_CR, CR_SLOTS, cunits)
        self.pg = 0
        if self.do_prologue:
            self.kf_dft_joint([self.filter_gen(j) for j in range(2)])
            if self.dbg == "kf":
                for j in range(2):
                    for fc in range(16):
                        b = Buf(self.arena, OFF_C + (j * 16 + fc) * 2048, BF16, (1024,))
                        self.dma("sp", b(), V(self.kf_d[j, fc], {("kfd", j, fc)}), P.dma_sem("dbgkf"))
                self.dump("kf", OFF_C)
        for s in range(self.nseq):
            if s == 0 or self.nlayers == 0:
                self.load_x(s)
            for l in range(self.nlayers):
                last = (l == self.nlayers - 1)
                if l % 2 == 0:
                    self.even_mixer(l)
                else:
                    self.odd_mixer(l)
                self.dump(f"z{l}", OFF_X32)
                self.layer_norm(("ln1g", l), ("ln1b", l), final=False)
                self.dump(f"ln1_{l}", OFF_X32)
                self.mlp(l)
                self.dump(f"zm{l}", OFF_X32)
                self.layer_norm(("ln2g", l), ("ln2b", l), final=last)
                self.dump(f"x{l}", OFF_X32)
            if self.nlayers > 0 and s + 1 < self.nseq:
                for o in range(8):
                    self.dma("pool", self.XB(o), self.DV(self.x_d[s + 1, :, o, :]), self.prog.dma_sem(f"xb{o}"))
            self.store_y(s)
            if self.nlayers > 0 and s + 1 < self.nseq:
                for o in range(8):
                    self.dma("sp", self.X32(o), self.DV(self.x_d[s + 1, :, o, :]), self.prog.dma_sem(f"xl{o}"))

    def nextg(self):
        g = self.pg
        self.pg ^= 1
        return g

    def load_x(self, s):
        for o in range(8):
            sem = self.prog.dma_sem(f"xl{o}")
            self.dma("sp", self.X32(o), self.DV(self.x_d[s, :, o, :]), sem)
        for o in range(8):
            self.copy("act" if o % 2 == 0 else "dve", self.XB(o), self.X32(o))

    def store_y(self, s):
        for o in range(8):
            sem = self.prog.dma_sem(f"ys{o}")
            self.dma("sp", V(self.y_d[s, :, o, :], {("ydram", s, o)}), self.X32(o), sem)

    def proj(self, wv, sub, src_chunks, g):
        nk = len(src_chunks)
        for kc in range(nk):
            for tt in range(4):
                self.mm(self.PS(4 * g + tt), wv(sub, kc), src_chunks[kc]((tt * 512, tt * 512 + 512)),
                        start=(kc == 0), stop=(kc == nk - 1))

    def wr_pair_iter(self, nchunks):
        cur = None
        for i in range(nchunks):
            if i % 2 == 0:
                cur = self.WR.acquire()
            slot = self.WR.slot(cur)
            wb = Buf(self.arena, slot.off, BF16, (2, 8, 128))
            yield i, wb, i % 2
            if i % 2 == 1:
                self.WR.release(cur)

    def even_mixer(self, l):
        j = l // 2
        ar = self.arena
        xb = [lambda r, o=o: self.XB(o, r) for o in range(8)]
        ya = Buf(ar, OFF_C2, BF16, (4, T))
        ac = Buf(ar, OFF_TT, BF16, (T,))
        tp = Buf(ar, OFF_C3, F32, (T,))
        cv = Buf(ar, OFF_C3 + 8192, F32, (T,))
        for i, wb, sub in self.wr_pair_iter(12):
            c, which = i // 3, i % 3
            g = self.nextg()
            self.proj(lambda s_, kc, wb=wb: wb(s_, kc), sub, xb, g)
            pg = self.PS(4 * g, 4)
            if which == 0:
                self.act(ac(), pg, AF.Copy)
            elif which == 1:
                self.tt("dve", tp(), pg, ac(), ALU.mult)
                self.ts("dve", cv(), tp(), self.col((("conva", j, 1), c)), ALU.mult)
                self.stt(cv((1, T)), tp((0, T - 1)), self.col((("conva", j, 0), c)), cv((1, T)), ALU.mult, ALU.add)
                self.stt(cv((0, T - 1)), tp((1, T)), self.col((("conva", j, 2), c)), cv((0, T - 1)), ALU.mult, ALU.add)
            else:
                self.tt("dve", ya(c), pg, cv(), ALU.mult)
        self.dump(f"ya{l}", OFF_C)
        x0 = Buf(ar, OFF_C0, BF16, (4, T))
        u = Buf(ar, OFF_C1, BF16, (4, T))
        tbs = [Buf(ar, OFF_C3, F32, (T,)), Buf(ar, OFF_C3 + 8192, F32, (T,))]
        x1c = Buf(ar, OFF_TT, BF16, (T,))
        for i, wb, sub in self.wr_pair_iter(12):
            c, which = i // 3, i % 3
            hc = {0: 4 + c, 1: 8 + c, 2: c}[which]
            g = self.nextg()
            self.proj(lambda s_, kc, wb=wb: wb(s_, kc), sub, xb, g)
            pg = self.PS(4 * g, 4)
            tb = tbs[i % 2]
            w0 = self.col((("shortw", j, 0), hc))
            w1 = self.col((("shortw", j, 1), hc))
            w2 = self.col((("shortw", j, 2), hc))
            bb = self.col((("shortb", j), hc))
            self.act(tb(), pg, AF.Identity, bias=bb, scale=w1)
            self.stt(tb((1, T)), self.PS(4 * g, 4, 0, T - 1), w0, tb((1, T)), ALU.mult, ALU.add)
            if which == 1:
                self.stt(tb((0, T - 1)), self.PS(4 * g, 4, 1, T), w2, tb((0, T - 1)), ALU.mult, ALU.add)
                self.tt("dve", u(c), tb(), x1c(), ALU.mult)
            else:
                dst = (lambda r: x1c(r)) if which == 0 else (lambda r, c=c: x0(c, r))
                self.stt(dst((0, T - 1)), self.PS(4 * g, 4, 1, T), w2, tb((0, T - 1)), ALU.mult, ALU.add)
                self.copy("act", dst((T - 1, T)), tb((T - 1, T)))
        self.dump(f"hy{l}", OFF_C)
        uT = Buf(ar, OFF_XB, BF16, (16, H))
        for tch in range(16):
            g = self.nextg()
            for c in range(4):
                self.transpose(self.PSB(4 * g, c * 128, (c + 1) * 128), u(c, (tch * 128, tch * 128 + 128)), self.IDN())
            self.copy("act" if tch % 2 == 0 else "dve", uT(tch), self.PSB(4 * g, 0, 512))
        Yre = Buf(ar, OFF_C1, BF16, (16, H))
        Yim = Buf(ar, OFF_C3, BF16, (16, H))
        tmp = Buf(ar, OFF_XB + 16384, F32, (6, H))
        kfbuf = [Buf(ar, OFF_XB + 16384 + 12288, BF16, (2, H)), Buf(ar, OFF_XB + 16384 + 12288 + 2048, BF16, (2, H))]
        self.fwd_dft(lambda kc: uT(kc), lambda kc: uT(kc), pointwise=(Yre, Yim, tmp, kfbuf, j))
        self.dump(f"fft{l}", OFF_C)
        yb = Buf(ar, OFF_XB, BF16, (4, T))
        for h in range(2):
            for fc in range(16):
                gi = self.CR.acquire()
                gs = self.CR.slot(gi)
                gb = Buf(ar, gs.off, BF16, (2, 1024))
                for i, Y in enumerate((Yre, Yim)):
                    for c in range(4):
                        for t2 in range(2):
                            self.mm(self.PS(2 * c + t2), Y(fc, (c * 128, c * 128 + 128)), gb(i, (t2 * 512, t2 * 512 + 512)),
                                    start=(fc == 0 and i == 0), stop=(fc == 15 and i == 1))
                self.CR.release(gi)
            for c in range(4):
                self.tt("dve", yb(c, (h * 1024, h * 1024 + 1024)), self.PS(2 * c, 2), x0(c, (h * 1024, h * 1024 + 1024)), ALU.mult)
        self.dump(f"yb{l}", OFF_XB)
        zb = Buf(ar, OFF_C0, BF16, (8, T))
        zq_lo = Buf(ar, OFF_C3, BF16, (4, T))
        zq_hi = Buf(ar, OFF_XB + 16384, BF16, (4, T))
        zbv = [lambda r, o=o: zb(o, r) for o in range(8)]
        zqv = [(lambda r, o=o: zq_lo(o, r)) if o < 4 else (lambda r, o=o: zq_hi(o - 4, r)) for o in range(8)]
        src = [lambda r, o=o: ya(o, r) for o in range(4)] + [lambda r, o=o: yb(o, r) for o in range(4)]
        self.out_proj(src, zbv, zqv, bias_key=None)
        self.ln_layout = (zbv, zqv, OFF_C2)

    def out_proj(self, src, zbv, zqv, bias_key):
        for i, wb, sub in self.wr_pair_iter(8):
            o = i
            g = self.nextg()
            self.proj(lambda s_, kc, wb=wb: wb(s_, kc), sub, src, g)
            pg = self.PS(4 * g, 4)
            self.stt(self.X32(o), self.X32(o), ALPHA, pg, ALU.mult, ALU.add)
            if bias_key is not None:
                self.ts("dve", self.X32(o), self.X32(o), self.col((bias_key, o)), ALU.add)
            self.act(zbv[o]((0, T)), self.X32(o), AF.Copy)
            self.act(zqv[o]((0, T)), self.X32(o), AF.Square)

    def fwd_dft(self, mov_re, mov_im, pointwise=None, kfgen=None):
        ar = self.arena
        def kfload(fc):
            kfbuf, j = pointwise[3], pointwise[4]
            dst = Buf(ar, kfbuf[fc % 2].off, BF16, (2 * H,))
            self.dma("sp", dst(), V(self.kf_d[j, fc], {("kfd", j, fc)}), self.prog.dma_sem(f"kfl{fc % 2}"))

        if pointwise is not None:
            kfload(0)
        for fc in range(16):
            if pointwise is not None:
                if fc + 1 < 16:
                    kfload(fc + 1)
                kfb = pointwise[3][fc % 2]
            g = self.nextg()
            for part, mov in enumerate((mov_re, mov_im)):
                fi = self.CR.acquire()
                fb = Buf(ar, self.CR.slot(fi).off, BF16, (16, 128))
                for kc in range(16):
                    self.mm(self.PS(4 * g + part), fb(kc), mov(kc), start=(kc == 0), stop=(kc == 15))
                self.CR.release(fi)
            pre, pim = self.PS(4 * g), self.PS(4 * g + 1)
            if pointwise is not None:
                Yre, Yim, tmp = pointwise[0], pointwise[1], pointwise[2]
                b = fc % 2
                ure, uim = tmp(b), tmp(2 + b)
                self.act(ure, pre, AF.Copy)
                self.act(uim, pim, AF.Copy)
                self.tt("dve", tmp(4), ure, kfb(0), ALU.mult)
                self.tt("dve", tmp(5), uim, kfb(1), ALU.mult)
                self.tt("dve", Yre(fc), tmp(4), tmp(5), ALU.subtract)
                self.tt("dve", tmp(4), ure, kfb(1), ALU.mult)
                self.tt("dve", tmp(5), uim, kfb(0), ALU.mult)
                self.tt("dve", Yim(fc), tmp(4), tmp(5), ALU.add)
            else:
                kfgen(fc, pre, pim)

    def ln_stats_tt(self, zbv, zqv, mean, rstd, tt):
        r = (tt * 512, tt * 512 + 512)
        for o in range(8):
            self.mm(self.PS(tt), self.ONESB(), zbv[o](r), start=(o == 0), stop=(o == 7))
        for o in range(8):
            self.mm(self.PS(4 + tt), self.ONESB(), zqv[o](r), start=(o == 0), stop=False)
        self.act(mean(r), self.PS(tt), AF.Copy)
        self.act(rstd(r), self.PS(tt), AF.Square)

    def ln_rstd_tt(self, rstd, tt):
        r = (tt * 512, tt * 512 + 512)
        self.mm(self.PS(4 + tt), self.NEGF(), rstd(r), start=False, stop=True)
        self.act(rstd(r), self.PS(4 + tt), AF.Ln, bias=EPS)
        self.act(rstd(r), rstd(r), AF.Exp, scale=-0.5)

    def ln_pipeline(self, zbv, zqv, mean, rstd, norm_fn):
        for tt in range(4):
            self.ln_stats_tt(zbv, zqv, mean, rstd, tt)
            if tt >= 1:
                self.ln_rstd_tt(rstd, tt - 1)
        self.ln_rstd_tt(rstd, 3)
        for tt in range(4):
            norm_fn(tt)
        self.pg = 0

    @staticmethod
    def bcn(v, n):
        return V(v.ap.unsqueeze(1).broadcast_to([128, n, 512]), v.res)

    def ln_center_scale(self, buf, mean, rstd, r, nd=6):
        xv = buf(None, r)
        self.tt("dve", xv, xv, self.bcn(mean(r), 8), ALU.subtract)
        self.tt("dve", xv, xv, self.bcn(rstd(r), 8), ALU.mult)

    def layer_norm(self, gkey, bkey, final):
        zbv, zqv, off_tmp = self.ln_layout
        mean = Buf(self.arena, off_tmp, F32, (T,))
        rstd = Buf(self.arena, off_tmp + 8192, F32, (T,))

        def norm(tt):
            r = (tt * 512, tt * 512 + 512)
            self.ln_center_scale(self.X32, mean, rstd, r)
            for o in range(8):
                g, b = self.col((gkey, o)), self.col((bkey, o))
                if not final:
                    self.act(self.XB(o, r), self.X32(o, r), AF.Identity, bias=b, scale=g)
                if o < 1 and not final:
                    self.ts("dve", self.X32(o, r), self.X32(o, r), g, ALU.mult, b, ALU.add)
                else:
                    self.act(self.X32(o, r), self.X32(o, r), AF.Identity, bias=b, scale=g)

        self.ln_pipeline(zbv, zqv, mean, rstd, norm)

    def mlp(self, l):
        ar = self.arena
        HID = Buf(ar, OFF_C, BF16, (16, T))
        xb = [lambda r, o=o: self.XB(o, r) for o in range(8)]
        zb = Buf(ar, OFF_XB, BF16, (8, T))
        zq = Buf(ar, OFF_C0, BF16, (8, T))
        for half in range(2):
            for i, wb, sub in self.wr_pair_iter(16):
                g = self.nextg()
                self.proj(lambda s_, kc, wb=wb: wb(s_, kc), sub, xb, g)
                pg = self.PS(4 * g, 4)
                self.act(HID(i), pg, AF.Square)
                self.stt(HID(i), pg, 0.0, HID(i), ALU.is_gt, ALU.mult)
            for o in range(8):
                wi = self.WR.acquire()
                wb = Buf(ar, self.WR.slot(wi).off, BF16, (16, 128))
                g = self.nextg()
                for kc in range(16):
                    for tt in range(4):
                        self.mm(self.PS(4 * g + tt), wb(kc), HID(kc, (tt * 512, tt * 512 + 512)), start=(kc == 0), stop=(kc == 15))
                self.WR.release(wi)
                pg = self.PS(4 * g, 4)
                if half == 0:
                    self.stt(self.X32(o), self.X32(o), ALPHA, pg, ALU.mult, ALU.add)
                else:
                    self.tt("dve", self.X32(o), self.X32(o), pg, ALU.add)
                    self.act(zb(o), self.X32(o), AF.Copy)
        for o in range(8):
            if o % 2 == 0:
                self.act(zq(o), self.X32(o), AF.Square)
            else:
                self.tt("dve", zq(o), zb(o), zb(o), ALU.mult)
        zbv = [lambda r, o=o: zb(o, r) for o in range(8)]
        zqv = [lambda r, o=o: zq(o, r) for o in range(8)]
        self.ln_layout = (zbv, zqv, OFF_C2)

    def odd_mixer(self, l):
        j = l // 2
        ar = self.arena
        xb = [lambda r, o=o: self.XB(o, r) for o in range(8)]
        H16 = Buf(ar, OFF_C0, BF16, (8, T))
        LB = [Buf(ar, OFF_TT, BF16, (32, 32)), Buf(ar, OFF_TT + 2048, BF16, (32, 32))]
        sg = Buf(ar, OFF_C2, F32, (T,))
        GW = T + 32
        glu = Buf(ar, OFF_C2 + 8192, BF16, (GW,))
        X4 = Buf(ar, OFF_C2 + 8192 + 4160, BF16, (4, GW))
        XW = T + 28
        self.memset("dve", glu((0, PADC)), 0.0)
        self.memset("dve", glu((T + PADC, GW)), 0.0)
        i4 = self.I4()
        i4_bc = V(i4.ap.unsqueeze(1).broadcast_to([128, 32, 32]), i4.res)
        x4sem = self.prog.dma_sem("x4")

        def build_diag(c):
            c0 = self.cols.idx[("dwq", j, c, 0, 0)]
            wv = self.KC((c0, c0 + 32))
            w_bc = V(wv.ap.unsqueeze(2).broadcast_to([128, 32, 32]), wv.res)
            self.tt("dve", LB[c % 2](), i4_bc, w_bc, ALU.mult)

        def spread(c):
            for jg in range(4):
                for sft in range(4):
                    self.dma("sp", X4(jg, (0, XW), parts=(32 * sft, 32 * sft + 32)),
                             glu((sft, sft + XW), parts=(32 * jg, 32 * jg + 32)), x4sem)

        def conv(c):
            lb = LB[c % 2]
            g2 = self.nextg()
            for q in range(8):
                for tt in range(4):
                    for jg in range(4):
                        self.mm(self.PS(4 * g2 + tt, parts=(32 * jg, 32 * jg + 32)), lb(jg * 8 + q),
                                X4(jg, (tt * 512 + 4 * q, tt * 512 + 4 * q + 512)),
                                start=(q == 0), stop=(q == 7), tile_position=(0, 32 * jg))
            self.act(H16(c), self.PS(4 * g2, 4), AF.Identity, bias=self.col((("dwb", j), c)))

        build_diag(0)
        it = self.wr_pair_iter(16)
        for step in range(16):
            i, wb, sub = next(it)
            c, which = i // 2, i % 2
            g = self.nextg()
            self.proj(lambda s_, kc, wb=wb: wb(s_, kc), sub, xb, g)
            pg = self.PS(4 * g, 4)
            if which == 0:
                self.act(sg(), pg, AF.Sigmoid, bias=self.col((("bpw1", j), 8 + c)))
            else:
                self.stt(glu((PADC, T + PADC)), pg, self.col((("bpw1", j), c)), sg(), ALU.add, ALU.mult)
                if c >= 1:
                    conv(c - 1)
                spread(c)
                if c + 1 < 8:
                    build_diag(c + 1)
        for _ in it:
            pass
        conv(7)
        hq = Buf(ar, OFF_XB, BF16, (8, T))
        for o in range(8):
            self.tt("dve", hq(o), H16(o), H16(o), ALU.mult)
        mean = Buf(ar, OFF_C2, F32, (T,))
        rstd = Buf(ar, OFF_C2 + 8192, F32, (T,))
        hbv = [lambda r, o=o: H16(o, r) for o in range(8)]
        hqv = [lambda r, o=o: hq(o, r) for o in range(8)]
        def cnorm(tt):
            r = (tt * 512, tt * 512 + 512)
            self.ln_center_scale(H16, mean, rstd, r, nd=5)
            for o in range(8):
                self.act(H16(o, r), H16(o, r), AF.Silu, scale=self.col((("olng", j), o)), bias=self.col((("olnb", j), o)))

        self.ln_pipeline(hbv, hqv, mean, rstd, cnorm)
        zb = Buf(ar, OFF_XB, BF16, (8, T))
        zq = Buf(ar, OFF_C2, BF16, (8, T))
        zbv = [lambda r, o=o: zb(o, r) for o in range(8)]
        zqv = [lambda r, o=o: zq(o, r) for o in range(8)]
        self.out_proj(hbv, zbv, zqv, bias_key=("bpw2", j))
        self.ln_layout = (zbv, zqv, OFF_C0)

    def filter_gen(self, j):
        ar = self.arena
        P = self.prog
        p64 = (0, 64)
        o = OFF_C
        zT = Buf(ar, o, F32, (T,)); o += 8192
        hA = Buf(ar, o, F32, (T,)); o += 8192
        hB = Buf(ar, o, F32, (T,)); o += 8192
        arg = Buf(ar, o, F32, (T,)); o += 8192
        msk = Buf(ar, o, F32, (T,)); o += 8192
        w4 = Buf(ar, o, F32, (1024,)); o += 4096
        w1 = Buf(ar, o, F32, (64,)); o += 256
        w2 = Buf(ar, o, F32, (64,)); o += 256
        w3 = Buf(ar, o, F32, (64,)); o += 256
        hyb = Buf(ar, o, F32, (H,)); o += 2048
        rn = Buf(ar, o, F32, (H,)); o += 2048
        kf = Buf(ar, o, F32, (H,)); o += 2048
        kb = Buf(ar, o, F32, (H,)); o += 2048
        akf = Buf(ar, o, F32, (H,)); o += 2048
        akb = Buf(ar, o, F32, (H,)); o += 2048
        A = Buf(ar, o, F32, (H,)); o += 2048
        t1 = Buf(ar, o, F32, (H,)); o += 2048
        assert o <= OFF_C + 65536
        if j == 0:
            S = Buf(ar, OFF_XB, BF16, (16, H))
            Dd = Buf(ar, OFF_XB + 16384, BF16, (16, H))
        else:
            S = Buf(ar, OFF_X32 + 40960, BF16, (16, H))
            Dd = Buf(ar, OFF_C + 24576, BF16, (16, H))
            hyb = Buf(ar, OFF_X32 + 57344, F32, (H,))
            rn = Buf(ar, OFF_X32 + 59392, F32, (H,))
        dec = Buf(ar, OFF_X32, F32, (16, H))
        stg = [Buf(ar, OFF_X32 + 32768 + (2 * j + b) * 2048, BF16, (2 * H,)) for b in range(2)]
        fsem = P.dma_sem("flt")
        if j == 0:
            self.dma("sp", zT(parts=(0, 33)), self.DV(self.zT_d), fsem)
            self.dma("sp", dec(), self.DV(self.dec_d), fsem)
        self.dma("sp", w1(parts=(0, 33)), self.DV(self.fw1_d[j]), P.dma_sem("flt1"))
        self.dma("sp", w2(parts=p64), self.DV(self.fw2_d[j]), P.dma_sem("flt2"))
        self.dma("sp", w3(parts=p64), self.DV(self.fw3_d[j]), P.dma_sem("flt3"))
        self.dma("sp", w4(parts=p64), self.DV(self.fw4_d[j]), P.dma_sem("flt4"))
        self.dma("sp", hyb(), self.DV(self.hyb_d[j].partition_broadcast(128)), P.dma_sem("flt5"))
        fr = self.col(("fr", j), parts=p64)
        PI = 3.14159
        TWO_PI = 2.0 * math.pi
        layers = [(w1, (0, 33), zT, "fb1", "frb1", hA), (w2, p64, hA, "fb2", "frb2", hB), (w3, p64, hB, "fb3", "frb3", hA)]
        for (w, kp, src, bn, fbn, dst) in layers:
            frb = self.col((fbn, j), parts=p64)
            self.tt("dve", frb, fr, self.col((bn, j), parts=p64), ALU.mult)
            g = self.nextg()
            for tt in range(4):
                self.mm(self.PS(4 * g + tt, parts=p64), w(parts=kp), src((tt * 512, tt * 512 + 512), parts=kp), start=True, stop=True)
            self.act(arg(parts=p64), self.PS(4 * g, 4, parts=p64), AF.Identity, bias=frb, scale=fr)
            for _ in range(2):
                self.ts("dve", msk(parts=p64), arg(parts=p64), PI, ALU.is_gt, -TWO_PI, ALU.mult)
                self.tt("dve", arg(parts=p64), arg(parts=p64), msk(parts=p64), ALU.add)
                self.ts("dve", msk(parts=p64), arg(parts=p64), -PI, ALU.is_lt, TWO_PI, ALU.mult)
                self.tt("dve", arg(parts=p64), arg(parts=p64), msk(parts=p64), ALU.add)
            self.act(dst(parts=p64), arg(parts=p64), AF.Sin)
        h3 = hA
        NB = 2
        kf2 = [kf, Buf(ar, OFF_X32 + 61440, F32, (H,))]
        kb2 = [kb, Buf(ar, OFF_X32 + 63488, F32, (H,))]
        akf2 = [akf, Buf(ar, OFF_TT, F32, (H,))]
        akb2 = [akb, Buf(ar, OFF_TT + 2048, F32, (H,))]
        for nch in range(16):
            b = nch % 2
            kf_, kb_, akf_, akb_ = kf2[b], kb2[b], akf2[b], akb2[b]
            g = self.nextg()
            lh = h3((nch * 128, nch * 128 + 128), parts=p64)
            self.mm(self.PS(4 * g), lh, w4((0, 512), parts=p64), start=True, stop=True)
            self.mm(self.PS(4 * g + 1), lh, w4((512, 1024), parts=p64), start=True, stop=True)
            self.tt("dve", kf_(), self.PS(4 * g), dec(nch), ALU.mult)
            self.tt("dve", kb_(), self.PS(4 * g + 1), dec(nch), ALU.mult)
            if nch == 0:
                self.memset("dve", kb_(parts=(0, 1)), 0.0)
            self.act(akf_(), kf_(), AF.Abs)
            self.act(akb_(), kb_(), AF.Abs)
            self.tt("dve", S(nch), kf_(), kb_(), ALU.add)
            self.tt("dve", Dd(nch), kf_(), kb_(), ALU.subtract)
            self.mm(self.PS(NB), self.ONESF(), akf_(), start=(nch == 0), stop=False)
            self.mm(self.PS(NB), self.ONESF(), akb_(), start=False, stop=(nch == 15))
        self.recip(rn(), self.PS(NB))

        def kfgen(fc, pre, pim):
            st_ = stg[fc % 2]
            self.tt("dve", t1(), pre, rn(), ALU.mult)
            self.tt("dve", st_((0, H)), t1(), hyb(), ALU.add)
            self.tt("dve", st_((H, 2 * H)), pim, rn(), ALU.mult)
            self.dma("sp", V(self.kf_d[j, fc], {("kfd", j, fc)}), st_(), P.dma_sem(f"kfw{j}{fc % 2}"))

        return {"S": S, "D": Dd, "kfgen": kfgen}

    def kf_dft_joint(self, lays):
        ar = self.arena
        for fc in range(16):
            g = self.nextg()
            for part in range(2):
                fi = self.CR.acquire()
                fb = Buf(ar, self.CR.slot(fi).off, BF16, (16, 128))
                for li, lay in enumerate(lays):
                    mov = lay["S"] if part == 0 else lay["D"]
                    for kc in range(16):
                        self.mm(self.PS(4 * g + 2 * li + part), fb(kc), mov(kc), start=(kc == 0), stop=(kc == 15))
                self.CR.release(fi)
            for li, lay in enumerate(lays):
                lay["kfgen"](fc, self.PS(4 * g + 2 * li), self.PS(4 * g + 2 * li + 1))


def host_inputs(inp):
    cols = build_cols(inp)
    zT, dec = filter_consts()
    common = {
        "wts": pack_weights(inp),
        "dft": dft_tables(),
        "cols": cols.arr(),
        "ident": np.eye(128, dtype=np.float32).astype(ml_dtypes.bfloat16),
        "i4": np.tile(np.eye(32, dtype=np.float32), (4, 1)).astype(ml_dtypes.bfloat16),
        "zT": zT,
        "dec": dec,
        "fw1": np.ascontiguousarray(inp["e_flt_w1"], np.float32),
        "fw2": np.ascontiguousarray(inp["e_flt_w2"], np.float32),
        "fw3": np.ascontiguousarray(inp["e_flt_w3"], np.float32),
        "fw4": np.ascontiguousarray(inp["e_flt_w4"], np.float32),
        "hyb": np.ascontiguousarray(inp["e_hy_bias"], np.float32).reshape(2, 1, H),
    }
    return cols, common


def to_dev_layout(x):
    n = x.shape[0]
    return np.ascontiguousarray(x.reshape(n, T, 8, 128).transpose(0, 3, 2, 1))


def from_dev_layout(y):
    n = y.shape[0]
    return np.ascontiguousarray(y.transpose(0, 3, 2, 1)).reshape(n, T, D)


def kernel(**inp):
    inp = {k: np.asarray(v) for k, v in inp.items()}
    cols, common = host_inputs(inp)
    nseq = NSEQ_TOTAL // NCORES
    kb = KB(nseq)
    nc = kb.build(cols)
    xs = np.concatenate([inp["x_prompt"], inp["x_sample"]], axis=0).astype(np.float32)
    in_maps = []
    for c in range(NCORES):
        m = dict(common)
        m["x"] = to_dev_layout(xs[c * nseq:(c + 1) * nseq])
        in_maps.append(m)
    res = run_bass_kernel_spmd(nc, in_maps, core_ids=list(range(NCORES)))
    ys = np.concatenate([from_dev_layout(np.asarray(res.results[c]["y"], np.float32)) for c in range(NCORES)], axis=0)
    nb = inp["x_prompt"].shape[0]
    return ys[:nb], ys[nb:]
```
